# Optimizing a Trainium2 kernel written in Bass

```python
import math
import jax, jax.numpy as jnp
from jax import lax
import numpy as np

D_MODEL = 1024
BATCH = 16
SEQ = 4096
DEPTH = 4

N_MIXERS = 2
N_META = 16
Q_BLOCK = 128
META_PAD = Q_BLOCK - N_META
EPS = 1e-6
NEG_BIG = -1e30

CONV_EXPAND = 2
CONV_WIDTH = CONV_EXPAND * D_MODEL
CONV_KERNEL = 31
CONV_IN = 3 * CONV_WIDTH

DIFF_HEADS = 8
DIFF_HEAD_DIM = D_MODEL // DIFF_HEADS // 2
DIFF_V_DIM = 2 * DIFF_HEAD_DIM
DIFF_QK_WIDTH = 2 * DIFF_HEADS * DIFF_HEAD_DIM
DIFF_WIDTH = DIFF_HEADS * DIFF_V_DIM
DIFF_IN = 2 * DIFF_QK_WIDTH + 2 * DIFF_WIDTH

N_CONV_LAYERS = (DEPTH + 1) // 2
N_ATTN_LAYERS = DEPTH // 2

kernel_name = "interleaved_conformer_conv_diff_attention_trunk"


def rms_norm(x, gain):
    xf = x.astype(jnp.float32)
    y = xf * lax.rsqrt(jnp.mean(xf * xf, axis=-1, keepdims=True) + EPS)
    return (y * gain.astype(jnp.float32)).astype(x.dtype)


def layer_norm(x, gain, bias):
    xf = x.astype(jnp.float32)
    mu = jnp.mean(xf, axis=-1, keepdims=True)
    xc = xf - mu
    var = jnp.mean(xc * xc, axis=-1, keepdims=True)
    y = xc * lax.rsqrt(var + EPS) * gain.astype(jnp.float32) + bias.astype(jnp.float32)
    return y.astype(x.dtype)


def conv_mixer(h, w_in, dw_kernel, dw_bias, ln_gain, ln_bias, w_out):
    proj = jnp.einsum("bld,dc->blc", h, w_in)
    u, g, z = jnp.split(proj, 3, axis=-1)
    v = u * jax.nn.sigmoid(g)
    v = lax.conv_general_dilated(
        v, dw_kernel[:, None, :].astype(v.dtype),
        window_strides=(1,), padding=[(CONV_KERNEL - 1, 0)],
        dimension_numbers=("NWC", "WIO", "NWC"),
        feature_group_count=CONV_WIDTH) + dw_bias
    v = jax.nn.silu(layer_norm(v, ln_gain, ln_bias))
    return jnp.einsum("ble,ed->bld", v * jax.nn.silu(z), w_out)


def diff_attention_mixer(h, w_in, lam_q1, lam_k1, lam_q2, lam_k2, subln_gain, w_out, lambda_init):
    b, l, _ = h.shape
    proj = jnp.einsum("bld,dc->blc", h, w_in)
    q, k, v, z = jnp.split(proj, [DIFF_QK_WIDTH, 2 * DIFF_QK_WIDTH,
                                  2 * DIFF_QK_WIDTH + DIFF_WIDTH], axis=-1)
    q = q.reshape(b, l, DIFF_HEADS, 2, DIFF_HEAD_DIM)
    k = k.reshape(b, l, DIFF_HEADS, 2, DIFF_HEAD_DIM)
    v = v.reshape(b, l, DIFF_HEADS, DIFF_V_DIM)
    pad5 = ((0, 0), (META_PAD, 0), (0, 0), (0, 0), (0, 0))
    q = jnp.pad(q, pad5)
    k = jnp.pad(k, pad5)
    v = jnp.pad(v, ((0, 0), (META_PAD, 0), (0, 0), (0, 0)))
    l_pad = l + META_PAD
    n_blocks = l_pad // Q_BLOCK
    pos = jnp.arange(l_pad, dtype=jnp.int32) - META_PAD

    lam = (jnp.exp(jnp.sum(lam_q1.astype(jnp.float32) * lam_k1.astype(jnp.float32)))
           - jnp.exp(jnp.sum(lam_q2.astype(jnp.float32) * lam_k2.astype(jnp.float32)))
           + lambda_init)
    slopes = jnp.exp2(-8.0 * jnp.arange(1, DIFF_HEADS + 1, dtype=jnp.float32) / DIFF_HEADS)
    scale = DIFF_HEAD_DIM ** -0.5

    outs = []
    for i in range(n_blocks):
        qs, qe = i * Q_BLOCK, (i + 1) * Q_BLOCK
        qb = q[:, qs:qe]
        kb = k[:, :qe]
        vb = v[:, :qe]
        s = jnp.einsum("bqhmd,bkhmd->bhmqk", qb, kb,
                       preferred_element_type=jnp.float32) * scale
        qpos = pos[qs:qe]
        kpos = pos[:qe]
        dist = (qpos[:, None] - kpos[None, :]).astype(jnp.float32)
        alibi = -slopes[:, None, None] * jnp.abs(dist)[None]
        valid = (kpos[None, :] <= qpos[:, None]) & (kpos[None, :] >= 0)
        s = jnp.where(valid, s + alibi[:, None], NEG_BIG)
        p = jax.nn.softmax(s, axis=-1)
        a = p[:, :, 0] - lam * p[:, :, 1]
        outs.append(jnp.einsum("bhqk,bkhe->bqhe", a.astype(vb.dtype), vb))
    o = jnp.concatenate(outs, axis=1)[:, META_PAD:]
    o = rms_norm(o, subln_gain) * (1.0 - lambda_init)
    o = o.reshape(b, l, DIFF_WIDTH) * jax.nn.silu(z)
    return jnp.einsum("blc,cd->bld", o, w_out)


def setup_inputs(seed: int = 0) -> dict:
    key = jax.random.key(seed)
    ks = jax.random.split(key, 20)
    f32 = jnp.float32
    nrm = lambda k, shape, s: jax.random.normal(k, shape, f32) * s
    return {
        "x": nrm(ks[0], (BATCH, SEQ, D_MODEL), 1.0),
        "meta_tokens": nrm(ks[1], (N_META, D_MODEL), 1.0),
        "norm_gain": 1.0 + nrm(ks[2], (DEPTH, D_MODEL), 0.02),
        "final_norm_gain": 1.0 + nrm(ks[3], (D_MODEL,), 0.02),
        "conv_w_in": nrm(ks[4], (N_CONV_LAYERS, D_MODEL, CONV_IN), D_MODEL ** -0.5),
        "conv_dw_kernel": nrm(ks[5], (N_CONV_LAYERS, CONV_KERNEL, CONV_WIDTH), CONV_KERNEL ** -0.5),
        "conv_dw_bias": nrm(ks[6], (N_CONV_LAYERS, CONV_WIDTH), 0.01),
        "conv_ln_gain": 1.0 + nrm(ks[7], (N_CONV_LAYERS, CONV_WIDTH), 0.02),
        "conv_ln_bias": nrm(ks[8], (N_CONV_LAYERS, CONV_WIDTH), 0.01),
        "conv_w_out": nrm(ks[9], (N_CONV_LAYERS, CONV_WIDTH, D_MODEL), CONV_WIDTH ** -0.5),
        "attn_w_in": nrm(ks[10], (N_ATTN_LAYERS, D_MODEL, DIFF_IN), D_MODEL ** -0.5),
        "attn_lambda_q1": nrm(ks[11], (N_ATTN_LAYERS, DIFF_HEAD_DIM), 0.1),
        "attn_lambda_k1": nrm(ks[12], (N_ATTN_LAYERS, DIFF_HEAD_DIM), 0.1),
        "attn_lambda_q2": nrm(ks[13], (N_ATTN_LAYERS, DIFF_HEAD_DIM), 0.1),
        "attn_lambda_k2": nrm(ks[14], (N_ATTN_LAYERS, DIFF_HEAD_DIM), 0.1),
        "attn_subln_gain": 1.0 + nrm(ks[15], (N_ATTN_LAYERS, DIFF_V_DIM), 0.02),
        "attn_w_out": nrm(ks[16], (N_ATTN_LAYERS, DIFF_WIDTH, D_MODEL), DIFF_WIDTH ** -0.5),
    }


def reference(x, meta_tokens, norm_gain, final_norm_gain,
              conv_w_in, conv_dw_kernel, conv_dw_bias, conv_ln_gain, conv_ln_bias, conv_w_out,
              attn_w_in, attn_lambda_q1, attn_lambda_k1, attn_lambda_q2, attn_lambda_k2,
              attn_subln_gain, attn_w_out):
    b = x.shape[0]
    meta = jnp.broadcast_to(meta_tokens[None].astype(x.dtype), (b, N_META, D_MODEL))
    h = jnp.concatenate([meta, x], axis=1)
    for i in range(DEPTH):
        hn = rms_norm(h, norm_gain[i])
        j = i // N_MIXERS
        if i % N_MIXERS == 0:
            y = conv_mixer(hn, conv_w_in[j], conv_dw_kernel[j], conv_dw_bias[j],
                           conv_ln_gain[j], conv_ln_bias[j], conv_w_out[j])
        else:
            lambda_init = 0.8 - 0.6 * math.exp(-0.3 * i)
            y = diff_attention_mixer(hn, attn_w_in[j], attn_lambda_q1[j], attn_lambda_k1[j],
                                     attn_lambda_q2[j], attn_lambda_k2[j], attn_subln_gain[j],
                                     attn_w_out[j], lambda_init)
        h = h + y
    return rms_norm(h, final_norm_gain)[:, N_META:]
```

```python
import math
from contextlib import ExitStack

import numpy as np
import concourse.bass as bass
import concourse.mybir as mybir
from concourse.bass_utils import run_bass_kernel_spmd

F32 = mybir.dt.float32
BF16 = mybir.dt.bfloat16
AF = mybir.ActivationFunctionType
ALU = mybir.AluOpType
AX = mybir.AxisListType

D = 1024
NMETA = 16
TS = 512
EPS = 1e-6
CW = 2048
KC = 31
NH = 8
MASKNEG = -30000.0
HALO = 32
SLAB = 4096
import os
PRODMODE = int(os.environ.get("PRODMODE", "0"))
DBG = int(os.environ.get("DBG", "0"))
GS = [128, 256, 512, 512, 512, 512, 512, 512]


class Eng:
    def __init__(self, nc, name, raw, selfsync):
        self.name = name
        self.raw = raw
        self.sem = nc.alloc_semaphore(name="sem_" + name)
        self.count = 0
        self.known = {}
        self.selfsync = selfsync


class Chan:
    def __init__(self, nc, name):
        self.name = name
        self.sem = nc.alloc_semaphore(name="ch_" + name)
        self.count = 0


class Buf:
    __slots__ = ("name", "w", "r")

    def __init__(self, name):
        self.name = name
        self.w = None
        self.r = {}


class K:
    def __init__(self, nc):
        self.nc = nc
        self.pe = Eng(nc, "pe", nc.tensor, False)
        self.act = Eng(nc, "act", nc.scalar, True)
        self.dve = Eng(nc, "dve", nc.vector, True)
        self.pool = Eng(nc, "pool", nc.gpsimd, True)
        self.sp = Eng(nc, "sp", nc.sync, False)
        self.engs = [self.pe, self.act, self.dve, self.pool, self.sp]
        self.chans = []

    def chan(self, name):
        c = Chan(self.nc, name)
        self.chans.append(c)
        return c

    def op(self, eng, fn, reads=(), writes=(), sig=True, chan=None, noself=False):
        need = {}
        for b in reads:
            if b.w is not None and need.get(b.w[0], 0) < b.w[1]:
                need[b.w[0]] = b.w[1]
        for b in writes:
            if b.w is not None and need.get(b.w[0], 0) < b.w[1]:
                need[b.w[0]] = b.w[1]
            for s, v in b.r.items():
                if need.get(s, 0) < v:
                    need[s] = v
        for src, val in need.items():
            if src is eng:
                if noself or not eng.selfsync or val > eng.count:
                    continue
            if eng.known.get(src, 0) >= val:
                continue
            eng.raw.wait_ge(src.sem, val)
            eng.known[src] = val
        ins = fn()
        if chan is not None:
            chan.count += 16
            ins.then_inc(chan.sem, 16)
            me = (chan, chan.count)
        elif sig:
            eng.count += 1
            ins.then_inc(eng.sem, 1)
            me = (eng, eng.count)
        else:
            me = (eng, eng.count + 1)
        for b in reads:
            if b.r.get(me[0], 0) < me[1]:
                b.r[me[0]] = me[1]
        for b in writes:
            b.w = me
            b.r = {}
        return ins

    def barrier(self):
        srcs = [(e, e.count) for e in self.engs if e.count > 0] + [(c, c.count) for c in self.chans if c.count > 0]
        for e in self.engs:
            for s, v in srcs:
                if s is e:
                    continue
                if e.known.get(s, 0) >= v:
                    continue
                e.raw.wait_ge(s.sem, v)
                e.known[s] = v


def cv_layout():
    off = {}
    o = 0
    for name, n in [("gT", 4 * 8), ("dwk", 2 * 16 * KC), ("dwb", 32), ("lng", 32), ("lnb", 32),
                    ("T1", NH * 36), ("T2", NH * 34), ("ident", 128), ("mask", 128)]:
        off[name] = (o, n)
        o += n
    return off, o


def cb_layout():
    off = {}
    o = 0
    for name, n in [("fg", D), ("lamv", 2 * 4 * 64)]:
        off[name] = (o, n)
        o += n
    return off, o


def host_consts(inp):
    off, ncv = cv_layout()
    cv = np.zeros((128, ncv), np.float32)
    p = np.arange(128)

    def put(name, arr):
        o, n = off[name]
        cv[:, o:o + n] = np.asarray(arr, np.float32).reshape(128, n)

    ng = np.asarray(inp["norm_gain"], np.float32)
    put("gT", ng.reshape(4, 8, 128).transpose(2, 0, 1))
    dk = np.asarray(inp["conv_dw_kernel"], np.float32)
    put("dwk", dk.reshape(2, KC, 16, 128).transpose(3, 0, 2, 1))
    for nm, key in [("dwb", "conv_dw_bias"), ("lng", "conv_ln_gain"), ("lnb", "conv_ln_bias")]:
        a = np.asarray(inp[key], np.float32)
        put(nm, a.reshape(2, 16, 128).transpose(2, 0, 1))
    slopes = 2.0 ** (-(np.arange(NH) + 1.0))
    d1 = np.arange(-32, 4)
    t1 = slopes[None, :, None] * (p[:, None, None] + 128.0 * d1[None, None, :])
    put("T1", t1)
    n2 = np.arange(34)
    t2 = slopes[None, :, None] * (p[:, None, None] - 16.0 - 128.0 * n2[None, None, :])
    t2[:, :, 33] = slopes[None, :] * p[:, None]
    put("T2", t2)
    put("ident", np.eye(128, dtype=np.float32))
    m = np.where(p[None, :] < p[:, None], MASKNEG, 0.0)
    put("mask", m)

    offb, ncb = cb_layout()
    cb = np.zeros((1, ncb), np.float32)
    o, n = offb["fg"]
    cb[0, o:o + n] = np.asarray(inp["final_norm_gain"], np.float32)
    o, n = offb["lamv"]
    lv = np.stack([np.asarray(inp[k], np.float32) for k in
                   ("attn_lambda_q1", "attn_lambda_k1", "attn_lambda_q2", "attn_lambda_k2")], axis=1)
    cb[0, o:o + n] = lv.reshape(-1)
    cb = np.ascontiguousarray(np.broadcast_to(cb, (128, ncb)))
    return cv, cb


def host_subln(inp):
    sg = np.asarray(inp["attn_subln_gain"], np.float32)
    row = np.tile(sg[:, None, :], (1, NH, 1)).reshape(1, -1)
    return np.ascontiguousarray(np.broadcast_to(row, (128, row.shape[1])))


def host_toeplitz(dw):
    dw = np.asarray(dw, np.float32)
    n = dw.shape[0]
    wc = dw.transpose(0, 2, 1)
    j = np.arange(32)[:, None]
    i = np.arange(32)[None, :]
    d = i - j
    k0 = 30 - d
    m0 = (d >= 0) & (d <= 30)
    k1 = -d - 2
    m1 = k1 >= 0
    T0 = np.where(m0[None, None], wc[:, :, np.clip(k0, 0, 30)], np.float32(0))
    T1 = np.where(m1[None, None], wc[:, :, np.clip(k1, 0, 30)], np.float32(0))
    T = np.stack([T0, T1], axis=2).astype(np.float32)
    T = T.reshape(n, 16, 4, 32, 2, 32, 32).transpose(0, 1, 2, 5, 3, 4, 6)
    return np.ascontiguousarray(T.reshape(n, 16, 128, 2048))


def build_program(n_seq=2, n_xt=8, depth=4, stop_after=None):
    SEQ = n_xt * TS
    L = NMETA + SEQ
    NKB = 1 + 4 * n_xt
    n_conv = (depth + 1) // 2
    n_attn = depth // 2
    nc = bass.Bass("TRN2", target_bir_lowering=False)
    offv, ncv = cv_layout()
    offb, ncb = cb_layout()

    x_d = nc.dram_tensor("x", [n_seq, SEQ, D], F32, kind="ExternalInput").ap()
    meta_d = nc.dram_tensor("meta", [NMETA, D], F32, kind="ExternalInput").ap()
    cv_d = nc.dram_tensor("cv", [128, ncv], F32, kind="ExternalInput").ap()
    cb_d = nc.dram_tensor("cb", [128, ncb], F32, kind="ExternalInput").ap()
    sg_d = nc.dram_tensor("sgb", [128, 2 * D], F32, kind="ExternalInput").ap()
    cwi_d = nc.dram_tensor("cwi", [n_conv, D, 3 * CW], F32, kind="ExternalInput").ap()
    cwo_d = nc.dram_tensor("cwo", [n_conv, CW, D], F32, kind="ExternalInput").ap()
    awi_d = nc.dram_tensor("awi", [max(n_attn, 1), D, 4 * D], F32, kind="ExternalInput").ap()
    awo_d = nc.dram_tensor("awo", [max(n_attn, 1), D, D], F32, kind="ExternalInput").ap()
    out_d = nc.dram_tensor("out", [n_seq, SEQ, D], F32, kind="ExternalOutput").ap()
    hbuf = nc.dram_tensor("hbuf", [n_seq, L, D], F32, kind="Internal").ap()
    NSLAB = 16 * n_conv + 10 * n_attn
    wscr = nc.dram_tensor("wscr", [NSLAB, 128, SLAB], BF16, kind="Internal").ap()
    vdram = nc.dram_tensor("vdram", [NKB, 128, NH, 129], BF16, kind="Internal").ap()
    hnd = nc.dram_tensor("hnd", [1 + n_xt, 128, 8, TS], BF16, kind="Internal").ap()
    tz_d = nc.dram_tensor("tz", [n_conv, 16, 128, 2048], F32, kind="ExternalInput").ap()
    wtoep = nc.dram_tensor("wtoep", [n_conv * 16, 128, 8192], BF16, kind="Internal").ap()

    k = K(nc)
    pe, act, dve, pool, sp = k.pe, k.act, k.dve, k.pool, k.sp
    op = k.op

    tiles = [(0, NMETA, 1, NMETA)] + [(NMETA + TS * i, TS, 4, 128) for i in range(n_xt)]

    def slab_ids_conv(lc):
        return [16 * lc + i for i in range(16)]

    def attn_base(la):
        return 16 * n_conv + 10 * la

    slab_seq = []
    for layer in range(depth):
        last = layer == depth - 1
        for s in range(n_seq):
            if layer % 2 == 0:
                for ti in range(len(tiles)):
                    slab_seq += slab_ids_conv(layer // 2)
            else:
                b = attn_base(layer // 2)
                for ti in range(len(tiles)):
                    slab_seq += [b + 2, b + 3, b + 4, b + 5]
                for ti in range(len(tiles)):
                    if last and ti == 0:
                        continue
                    slab_seq += [b + 0, b + 1, b + 6, b + 7, b + 8, b + 9]

    toep_seq = []
    for layer in range(0, depth, 2):
        for s_ in range(n_seq):
            for ti in range(1, len(tiles)):
                toep_seq += [16 * (layer // 2) + j for j in range(16)]

    with ExitStack() as es:
        def sbuf(name, shape, dt, stack=None):
            return (stack or es).enter_context(nc.sbuf_tensor(name, shape, dt))

        ps_all = es.enter_context(nc.psum_tensor("ps_all", [128, 8, 512], F32))
        banks = [ps_all[:, i, :] for i in range(8)]
        bankb = [Buf(f"bank{i}") for i in range(8)]
        cv = sbuf("cv_sb", [128, ncv], F32)
        cvb = Buf("cv")
        cbt = sbuf("cb_sb", [128, ncb], F32)
        cbb = Buf("cb")
        identb = sbuf("identb", [128, 128], BF16)
        maskb = sbuf("maskb", [128, 128], BF16)
        constb = Buf("constb")
        hT = sbuf("hT", [128, 4, D], F32)
        hTb = Buf("hT")
        tokA = sbuf("tokA", [128, 4, D], BF16)
        tokAb = Buf("tokA")
        hnT = sbuf("hnT", [128, 8, TS], BF16)
        hnTb = Buf("hnT")
        junk = sbuf("junk", [128, D], BF16)
        junkb = Buf("junk")
        small = sbuf("small", [128, 64], F32)
        ss_b, rs_b, tmp_b = Buf("ss"), Buf("rs"), Buf("tmp")
        NSLOT = 2
        slabs = [sbuf(f"slab{i}", [128, SLAB], BF16) for i in range(NSLOT)]
        slabb = [Buf(f"slab{i}") for i in range(NSLOT)]
        slabch = [k.chan(f"slab{i}") for i in range(NSLOT)]
        ch_h = k.chan("hload")
        ch_st = k.chan("hstore")
        ch_c = k.chan("const")
        ch_c2 = k.chan("const2")
        wscrb = [Buf(f"wscr{i}") for i in range(NSLAB)]
        wtoepb = [Buf(f"wtoep{i}") for i in range(n_conv * 16)]
        hbufb = [[Buf(f"hbuf{s}_{t}") for t in range(len(tiles))] for s in range(n_seq)]
        outb = Buf("out")

        def cvs(name, i0=0, n=None):
            o, tot = offv[name]
            n = tot if n is None else n
            return cv[:, o + i0:o + i0 + n]

        def cbs(name, i0=0, n=None):
            o, tot = offb[name]
            n = tot if n is None else n
            return cbt[:, o + i0:o + i0 + n]

        bank_rr = [0]

        bank_live = set()

        def next_bank(pool_ids=range(8), claim=False):
            ids = list(pool_ids)
            i = ids[bank_rr[0] % len(ids)]
            bank_rr[0] += 1
            assert i not in bank_live, ("PSUM bank still live", i)
            if claim:
                bank_live.add(i)
            return banks[i], bankb[i]

        def bank_release(bb):
            bank_live.discard(bankb.index(bb))

        op(sp, lambda: sp.raw.dma_start(out=cv[:], in_=cv_d[:, :]), writes=[cvb], chan=ch_c)
        op(sp, lambda: sp.raw.dma_start(out=cbt[:], in_=cb_d[:, :]), writes=[cbb], chan=ch_c2)
        op(dve, lambda: dve.raw.tensor_copy(out=identb[:], in_=cvs("ident")), reads=[cvb], writes=[constb])
        op(dve, lambda: dve.raw.tensor_copy(out=maskb[:], in_=cvs("mask")), reads=[cvb], writes=[constb])

        with ExitStack() as ps_:
            NST = 3
            stg = [sbuf(f"stg{i}", [128, SLAB], F32, ps_) for i in range(NST)]
            stgb = [Buf(f"stg{i}") for i in range(NST)]
            stgch = [k.chan(f"stg{i}") for i in range(NST)]
            sto = [sbuf(f"sto{i}", [128, SLAB], BF16, ps_) for i in range(2)]
            stob = [Buf(f"sto{i}") for i in range(2)]
            stoch = [k.chan(f"sto{i}") for i in range(2)]
            tf = [sbuf(f"tf{i}", [128, 64, 128], BF16, ps_) for i in range(2)]
            tfb = [Buf(f"tf{i}") for i in range(2)]
            tfch = [k.chan(f"tf{i}") for i in range(2)]
            for i in range(2):
                op(pool, lambda i=i: pool.raw.memset(tf[i][:, :, :], 0.0), writes=[tfb[i]])
            units = []

            for lc in range(n_conv):
                wi = cwi_d[lc].rearrange("(a p) c -> p a c", p=128)
                wo = cwo_d[lc].rearrange("(a p) c -> p a c", p=128)
                for s_ in range(8):
                    units.append(("w", 16 * lc + s_, [(wi[:, :, 256 * s_:256 * s_ + 256], 8, 0, 256),
                                                      (wi[:, :, CW + 256 * s_:CW + 256 * s_ + 256], 8, 256, 256)]))
                for s_ in range(4):
                    units.append(("w", 16 * lc + 8 + s_, [(wi[:, :, 2 * CW + 512 * s_:2 * CW + 512 * s_ + 512], 8, 0, 512)]))
                for s_ in range(4):
                    units.append(("w", 16 * lc + 12 + s_, [(wo[:, :, 256 * s_:256 * s_ + 256], 16, 0, 256)]))
                for cc in range(16):
                    units.append(("t", lc * 16 + cc, (lc, cc)))
            for la in range(n_attn):
                wi = awi_d[la].rearrange("(a p) c -> p a c", p=128)
                wo = awo_d[la].rearrange("(a p) c -> p a c", p=128)
                b = attn_base(la)
                for s_ in range(8):
                    units.append(("w", b + s_, [(wi[:, :, 512 * s_:512 * s_ + 512], 8, 0, 512)]))
                for s_ in range(2):
                    units.append(("w", b + 8 + s_, [(wo[:, :, 512 * s_:512 * s_ + 512], 8, 0, 512)]))

            def u_load(n):
                kind, sid, info = units[n]
                i = n % NST
                if kind == "w":
                    for (src, A, w0, W) in info:
                        dst = stg[i][:, :].rearrange("p (a w) -> p a w", a=A)[:, :, w0:w0 + W]
                        op(sp, lambda dst=dst, src=src: sp.raw.dma_start(out=dst, in_=src),
                           writes=[stgb[i]], chan=stgch[i])
                else:
                    lc, cc = info
                    op(sp, lambda: sp.raw.dma_start(out=stg[i][:, 0:2048], in_=tz_d[lc, cc, :, :]),
                       writes=[stgb[i]], chan=stgch[i])

            wcnt, tcnt = [0], [0]

            def u_cast_store(n):
                kind, sid, info = units[n]
                i = n % NST
                if kind == "w":
                    o = wcnt[0] % 2
                    wcnt[0] += 1
                    if n % 2 == 0:
                        op(act, lambda: act.raw.activation(out=sto[o][:], in_=stg[i][:], func=AF.Copy),
                           reads=[stgb[i]], writes=[stob[o]])
                    else:
                        op(dve, lambda: dve.raw.tensor_copy(out=sto[o][:], in_=stg[i][:]),
                           reads=[stgb[i]], writes=[stob[o]])
                    op(sp, lambda: sp.raw.dma_start(out=wscr[sid, :, :], in_=sto[o][:]),
                       reads=[stob[o]], writes=[wscrb[sid]], chan=stoch[o])
                else:
                    o = tcnt[0] % 2
                    tcnt[0] += 1
                    for q in range(4):
                        src = stg[i][32 * q:32 * q + 32, 0:2048].rearrange("p (a i) -> p a i", i=32)
                        dst = tf[o][32 * q:32 * q + 32, :, 32 * q:32 * q + 32]
                        if q % 2 == 0:
                            op(dve, lambda src=src, dst=dst: dve.raw.tensor_copy(out=dst, in_=src),
                               reads=[stgb[i]], writes=[tfb[o]])
                        else:
                            op(act, lambda src=src, dst=dst: act.raw.activation(out=dst, in_=src, func=AF.Copy),
                               reads=[stgb[i]], writes=[tfb[o]])
                    op(sp, lambda: sp.raw.dma_start(out=wtoep[sid, :, :],
                                                    in_=tf[o][:, :, :].rearrange("p a m -> p (a m)")),
                       reads=[tfb[o]], writes=[wtoepb[sid]], chan=tfch[o])

            u_load(0)
            if len(units) > 1:
                u_load(1)
            for n in range(len(units)):
                if n + 2 < len(units):
                    u_load(n + 2)
                u_cast_store(n)
            k.barrier()

        sl_state = {"issued": 0, "taken": 0}

        def slab_issue():
            i = sl_state["issued"]
            if i >= len(slab_seq):
                return
            slot = i % NSLOT
            sid = slab_seq[i]
            op(sp, lambda: sp.raw.dma_start(out=slabs[slot][:], in_=wscr[sid, :, :]),
               reads=[wscrb[sid]], writes=[slabb[slot]], chan=slabch[slot])
            sl_state["issued"] += 1

        def slab_take(expect):
            i = sl_state["taken"]
            assert slab_seq[i] == expect, (i, slab_seq[i], expect)
            while sl_state["issued"] < min(i + NSLOT, len(slab_seq)):
                slab_issue()
            sl_state["taken"] += 1
            slot = i % NSLOT
            return slabs[slot], slabb[slot]

        tp_state = {"issued": 0, "taken": 0, "slabs": None, "bufs": None, "chans": None, "limit": 0}

        def toep_issue():
            i = tp_state["issued"]
            if i >= len(toep_seq):
                return
            slot = i % 2
            tid = toep_seq[i]
            op(sp, lambda: sp.raw.dma_start(out=tp_state["slabs"][slot][:], in_=wtoep[tid, :, :]),
               reads=[wtoepb[tid]], writes=[tp_state["bufs"][slot]], chan=tp_state["chans"][slot])
            tp_state["issued"] += 1

        def toep_take(expect):
            i = tp_state["taken"]
            assert toep_seq[i] == expect, (i, toep_seq[i], expect)
            while tp_state["issued"] < min(i + 2, tp_state["limit"]):
                toep_issue()
            tp_state["taken"] += 1
            return tp_state["slabs"][i % 2], tp_state["bufs"][i % 2]

        cur = {"hT": hT, "hTb": hTb, "tokA": tokA, "tokAb": tokAb, "hnT": hnT, "hnTb": hnTb, "hch": ch_h, "sch": ch_st}

        def rstd_from(ss_ap, out_ap, n, rd, wr):
            t = small[:ss_ap.shape[0], 32:32 + ss_ap.shape[1]]
            op(dve, lambda: dve.raw.tensor_scalar(out=t, in0=ss_ap, scalar1=1.0 / n, scalar2=EPS,
                                                  op0=ALU.mult, op1=ALU.add), reads=[rd], writes=[tmp_b])
            op(act, lambda: act.raw.activation(out=t, in_=t, func=AF.Ln), reads=[tmp_b], writes=[tmp_b])
            op(act, lambda: act.raw.activation(out=out_ap, in_=t, func=AF.Exp, scale=-0.5),
               reads=[tmp_b], writes=[wr])

        def load_h(layer, s, ti):
            hT, hTb = cur["hT"], cur["hTb"]
            t0, N, nsb, rows = tiles[ti]
            if layer == 0:
                if ti == 0:
                    src = meta_d[:, :]
                    dst = hT[:NMETA, 0, :]
                else:
                    src = x_d[s, t0 - NMETA:t0 - NMETA + N, :].rearrange("(s p) d -> p s d", p=128)
                    dst = hT[:, :, :]
                op(pool, lambda: pool.raw.dma_start(out=dst, in_=src), writes=[hTb], chan=cur["hch"])
            else:
                if ti == 0:
                    src = hbuf[s, 0:NMETA, :]
                    dst = hT[:NMETA, 0, :]
                else:
                    src = hbuf[s, t0:t0 + N, :].rearrange("(s p) d -> p s d", p=128)
                    dst = hT[:, :, :]
                op(pool, lambda: pool.raw.dma_start(out=dst, in_=src), reads=[hbufb[s][ti]], writes=[hTb], chan=cur["hch"])

        def store_h(s, ti):
            hT, hTb = cur["hT"], cur["hTb"]
            t0, N, nsb, rows = tiles[ti]
            if ti == 0:
                dst = hbuf[s, 0:NMETA, :]
                src = hT[:NMETA, 0, :]
            else:
                dst = hbuf[s, t0:t0 + N, :].rearrange("(s p) d -> p s d", p=128)
                src = hT[:, :, :]
            op(pool, lambda: pool.raw.dma_start(out=dst, in_=src), reads=[hTb], writes=[hbufb[s][ti]], chan=cur["sch"])

        def norm_T(layer, ti, act_ok=True, pool_ids=range(8)):
            hT, hTb, tokA, tokAb = cur["hT"], cur["hTb"], cur["tokA"], cur["tokAb"]
            hnT, hnTb = cur["hnT"], cur["hnTb"]
            t0, N, nsb, rows = tiles[ti]
            ss = small[:, 0:4]
            rs = small[:, 4:8]
            for sb in range(nsb):
                op(act, lambda sb=sb: act.raw.activation(out=junk[:rows, :], in_=hT[:rows, sb, :], func=AF.Square,
                                                         accum_out=ss[:rows, sb:sb + 1]),
                   reads=[hTb], writes=[junkb, ss_b])
            rstd_from(ss[:rows, 0:nsb], rs[:rows, 0:nsb], float(D), ss_b, rs_b)
            for sb in range(nsb):
                if sb % 2 == 0 or not act_ok:
                    op(dve, lambda sb=sb: dve.raw.tensor_scalar(out=tokA[:rows, sb, :], in0=hT[:rows, sb, :],
                                                                scalar1=rs[:rows, sb:sb + 1], scalar2=None,
                                                                op0=ALU.mult),
                       reads=[hTb, rs_b], writes=[tokAb])
                else:
                    op(act, lambda sb=sb: act.raw.activation(out=tokA[:rows, sb, :], in_=hT[:rows, sb, :],
                                                             func=AF.Copy, scale=rs[:rows, sb:sb + 1]),
                       reads=[hTb, rs_b], writes=[tokAb])
            transpose_to(hnT, hnTb, layer, N, nsb, rows, pool_ids)

        def transpose_to(dst, dstb, layer, N, nsb, rows, pool_ids=range(8)):
            tokA, tokAb = cur["tokA"], cur["tokAb"]
            for c in range(8):
                bk, bb = next_bank(pool_ids)
                pT = bk.bitcast(BF16)
                for sb in range(nsb):
                    op(pe, lambda sb=sb: pe.raw.transpose(out=pT[:, sb * 128:sb * 128 + rows],
                                                          in_=tokA[:rows, sb, c * 128:(c + 1) * 128],
                                                          identity=identb[:rows, :rows]),
                       reads=[tokAb, constb], writes=[bb], sig=(sb == nsb - 1))
                if layer is None:
                    if c % 2 == 1:
                        op(act, lambda: act.raw.activation(out=dst[:, c, :N], in_=pT[:, :N], func=AF.Copy),
                           reads=[bb], writes=[dstb])
                    else:
                        op(dve, lambda: dve.raw.tensor_copy(out=dst[:, c, :N], in_=pT[:, :N]),
                           reads=[bb], writes=[dstb])
                else:
                    g = cvs("gT", layer * 8 + c, 1)
                    if c % 2 == 0:
                        op(dve, lambda: dve.raw.tensor_scalar(out=dst[:, c, :N], in0=pT[:, :N], scalar1=g,
                                                              scalar2=None, op0=ALU.mult),
                           reads=[bb, cvb], writes=[dstb])
                    else:
                        op(act, lambda: act.raw.activation(out=dst[:, c, :N], in_=pT[:, :N], func=AF.Copy, scale=g),
                           reads=[bb, cvb], writes=[dstb])

        def final_out(s, ti, fo, fob, foch):
            hT, hTb = cur["hT"], cur["hTb"]
            t0, N, nsb, rows = tiles[ti]
            ss = small[:, 8:12]
            rs = small[:, 12:16]
            for sb in range(nsb):
                op(act, lambda sb=sb: act.raw.activation(out=junk[:rows, :], in_=hT[:rows, sb, :], func=AF.Square,
                                                         accum_out=ss[:rows, sb:sb + 1]),
                   reads=[hTb], writes=[junkb, ss_b])
            rstd_from(ss[:rows, 0:nsb], rs[:rows, 0:nsb], float(D), ss_b, rs_b)
            for sb in range(nsb):
                op(dve, lambda sb=sb: dve.raw.scalar_tensor_tensor(out=fo[:rows, sb, :], in0=hT[:rows, sb, :],
                                                                   scalar=rs[:rows, sb:sb + 1], in1=cbs("fg")[:rows, :],
                                                                   op0=ALU.mult, op1=ALU.mult),
                   reads=[hTb, rs_b, cbb], writes=[fob])
            dst = out_d[s, t0 - NMETA:t0 - NMETA + N, :].rearrange("(s p) d -> p s d", p=128)
            op(pool, lambda: pool.raw.dma_start(out=dst, in_=fo[:, :, :]), reads=[fob], writes=[outb], chan=foch)

        def conv_layer(layer):
            lc = layer // 2
            last = layer == depth - 1
            base = 16 * lc
            with ExitStack() as ls:
                v = sbuf(f"cv_v_L{layer}", [128, 16, HALO + TS], BF16, ls)
                vb = [Buf(f"v{j}") for j in range(16)]
                vhalo = sbuf(f"cv_halo_L{layer}", [128, 16, HALO], BF16, ls)
                tsl = [sbuf(f"cv_tsl{i}_L{layer}", [128, 8192], BF16, ls) for i in range(2)]
                tp_state["slabs"] = tsl
                tp_state["bufs"] = [Buf(f"tsl{i}") for i in range(2)]
                tp_state["chans"] = [k.chan(f"tsl{i}_{layer}") for i in range(2)]
                tp_state["limit"] = (lc + 1) * n_seq * (len(tiles) - 1) * 16
                vt = [sbuf(f"cv_vt{i}_L{layer}", [128, HALO + TS], BF16, ls) for i in range(2)]
                vtb = [Buf(f"vt{i}") for i in range(2)]
                pk = [sbuf(f"cv_pk{i}_L{layer}", [128, 32], BF16, ls) for i in range(8)]
                pkb = [Buf(f"pk{i}") for i in range(8)]
                pkr = [0]
                vhb = Buf("vhalo")
                co = sbuf(f"cv_co_L{layer}", [128, 16, TS], F32, ls)
                cob = [Buf(f"co{j}") for j in range(16)]
                sz = sbuf(f"cv_sz_L{layer}", [128, 16, TS], BF16, ls)
                szb = [Buf(f"sz{j}") for j in range(16)]
                tmpf = [sbuf(f"cv_t{i}_L{layer}", [128, TS], F32, ls) for i in range(3)]
                tmpfb = [Buf(f"cv_t{i}") for i in range(3)]
                xb = [sbuf(f"cv_xb{i}_L{layer}", [128, 2, TS], BF16, ls) for i in range(2)]
                xbb = [Buf(f"cv_xb{i}") for i in range(2)]
                ones = sbuf(f"cv_ones_L{layer}", [128, 128], BF16, ls)
                onesb = Buf("ones")
                stA = sbuf(f"cv_stA_L{layer}", [128, TS], F32, ls)
                stB = sbuf(f"cv_stB_L{layer}", [128, TS], F32, ls)
                stT = sbuf(f"cv_stT_L{layer}", [128, TS], F32, ls)
                stb = Buf("stAB")
                sttb = Buf("stT")
                fo = fob = foch = None
                if last:
                    fo, fob = hT, hTb
                    foch = k.chan("fo_c%d" % layer)
                op(pool, lambda: pool.raw.memset(ones[:], 1.0), writes=[onesb])
                hT2 = sbuf(f"cv_hT2_L{layer}", [128, 4, D], F32, ls)
                hnT2 = sbuf(f"cv_hnT2_L{layer}", [128, 8, TS], BF16, ls)
                sets = [dict(cur),
                        {"hT": hT2, "hTb": Buf("hT2"), "tokA": tokA, "tokAb": tokAb, "hnT": hnT2, "hnTb": Buf("hnT2"),
                         "hch": k.chan(f"hload2_{layer}"), "sch": k.chan(f"hstore2_{layer}")}]
                all_tiles = [(s_, ti_) for s_ in range(n_seq) for ti_ in range(len(tiles))]

                def prep(idx_):
                    s_, ti_ = all_tiles[idx_]
                    cur.update(sets[idx_ % 2])
                    load_h(layer, s_, ti_)
                    norm_T(layer, ti_, pool_ids=range(6))

                prep(0)
                tile_idx = [0]
                trr = [0]

                def next_tmp():
                    i = trr[0] % 3
                    trr[0] += 1
                    return tmpf[i], tmpfb[i]

                for s in range(n_seq):
                    prevN = None
                    for ti in range(len(tiles)):
                        t0, N, nsb, rows = tiles[ti]
                        my_idx = tile_idx[0]
                        tile_idx[0] += 1
                        st_ = sets[my_idx % 2]
                        chT, chTb, chn, chnb = st_["hT"], st_["hTb"], st_["hnT"], st_["hnTb"]
                        if prevN is None:
                            op(pool, lambda: pool.raw.memset(v[:, :, 0:HALO], 0.0), writes=vb)
                        else:
                            op(pool, lambda pn=prevN: pool.raw.tensor_copy(out=vhalo[:, :, :], in_=v[:, :, pn:pn + HALO]),
                               reads=vb, writes=[vhb])
                            op(pool, lambda: pool.raw.tensor_copy(out=v[:, :, 0:HALO], in_=vhalo[:, :, :]),
                               reads=[vhb], writes=vb)
                        prevN = N
                        ps_s, ps_sb = banks[6], bankb[6]
                        ps_q, ps_qb = banks[7], bankb[7]

                        def conv_mm_prod(j):
                            pc, pcb = next_bank(range(6), claim=True)
                            for kk in range(KC):
                                w = cvs("dwk", (lc * 16 + j) * KC + kk, 1)
                                pi = pkr[0] % len(pk)
                                pkr[0] += 1
                                o0 = kk + HALO - 30
                                e0 = o0 & ~1
                                wn = min(N + 2, HALO + N - e0)
                                if (kk % 5 in (1, 3)) if PRODMODE == 0 else (PRODMODE == 1):
                                    op(act, lambda pi=pi, w=w, e0=e0, wn=wn: act.raw.activation(
                                        out=pk[pi][:, :wn], in_=v[:, j, e0:e0 + wn], func=AF.Copy, scale=w),
                                       reads=[vb[j], cvb], writes=[pkb[pi]])
                                else:
                                    op(dve, lambda pi=pi, w=w, e0=e0, wn=wn: dve.raw.tensor_scalar(
                                        out=pk[pi][:, :wn], in0=v[:, j, e0:e0 + wn], scalar1=w, scalar2=None,
                                        op0=ALU.mult),
                                       reads=[vb[j], cvb], writes=[pkb[pi]])
                                op(pe, lambda pi=pi, kk=kk, e0=e0, o0=o0: pe.raw.matmul(pc[:, :N], lhsT=identb[:, :],
                                                                                 rhs=pk[pi][:, o0 - e0:o0 - e0 + N],
                                                                                 start=(kk == 0), stop=(kk == KC - 1)),
                                   reads=[pkb[pi], constb], writes=[pcb], sig=True)
                            return pc, pcb

                        def conv_tin(j):
                            vi = j % 2
                            op(dve, lambda: dve.raw.transpose(out=vt[vi][:, :], in_=v[:, j, :]),
                               reads=[vb[j]], writes=[vtb[vi]])

                        def conv_mm_toep(j):
                            vi = j % 2
                            tslab, tslb = toep_take(lc * 16 + j)
                            tv = tslab[:, :].rearrange("p (c w m) -> p c w m", c=32, w=2)
                            pc, pcb = next_bank(range(6), claim=True)
                            pcv = pc.rearrange("p (n c) -> p n c", c=32)
                            vtv = vt[vi][:, :].rearrange("p (n c) -> p n c", c=32)
                            for cl in range(32):
                                op(pe, lambda cl=cl: pe.raw.matmul(pcv[:, :, cl], lhsT=tv[:, cl, 0, :],
                                                                   rhs=vtv[:, 1:17, cl], start=True, stop=False),
                                   reads=[tslb, vtb[vi]], writes=[pcb], sig=False)
                                op(pe, lambda cl=cl: pe.raw.matmul(pcv[:, :, cl], lhsT=tv[:, cl, 1, :],
                                                                   rhs=vtv[:, 0:16, cl], start=False, stop=True),
                                   reads=[tslb, vtb[vi]], writes=[pcb], sig=(cl == 31))
                            return pc, pcb

                        def conv_evac(j, pc, pcb, toep):
                            bank_release(pcb)
                            bsc = cvs("dwb", lc * 16 + j, 1)
                            xi = j % 2
                            if toep:
                                op(dve, lambda: dve.raw.transpose(out=co[:, j, :], in_=pc[:, :]),
                                   reads=[pcb], writes=[cob[j]])
                                op(dve, lambda: dve.raw.tensor_scalar(out=co[:, j, :N], in0=co[:, j, :N], scalar1=bsc,
                                                                      scalar2=None, op0=ALU.add),
                                   reads=[cob[j], cvb], writes=[cob[j]])
                            else:
                                op(dve, lambda: dve.raw.tensor_scalar(out=co[:, j, :N], in0=pc[:, :N], scalar1=bsc,
                                                                      scalar2=None, op0=ALU.add),
                                   reads=[pcb, cvb], writes=[cob[j]])
                            op(act, lambda: act.raw.activation(out=xb[xi][:, 0, :N], in_=co[:, j, :N], func=AF.Copy),
                               reads=[cob[j]], writes=[xbb[xi]])
                            op(act, lambda: act.raw.activation(out=xb[xi][:, 1, :N], in_=co[:, j, :N], func=AF.Square),
                               reads=[cob[j]], writes=[xbb[xi]])
                            op(pe, lambda: pe.raw.matmul(ps_s[:, :N], lhsT=ones[:, :], rhs=xb[xi][:, 0, :N],
                                                         start=(j == 0), stop=(j == 15)),
                               reads=[onesb, xbb[xi]], writes=[ps_sb], sig=False)
                            op(pe, lambda: pe.raw.matmul(ps_q[:, :N], lhsT=ones[:, :], rhs=xb[xi][:, 1, :N],
                                                         start=(j == 0), stop=(j == 15)),
                               reads=[onesb, xbb[xi]], writes=[ps_qb], sig=True)


                        use_toep = ti > 0 and DBG not in (5, 6)
                        pend_ev = []

                        def conv_chunk(j):
                            if use_toep:
                                conv_tin(j)
                                pc, pcb = conv_mm_toep(j)
                            else:
                                pc, pcb = conv_mm_prod(j)
                            while pend_ev:
                                conv_evac(*pend_ev.pop(0))
                            pend_ev.append((j, pc, pcb, use_toep))
                            if DBG == 6:
                                conv_evac(*pend_ev.pop(0))

                        pend = {}

                        slab_cur = {}

                        def proj_chunk(j):
                            sp_i, jj = j // 2, j % 2
                            if jj == 0:
                                slab, slb = slab_take(base + sp_i)
                                slab_cur["sl"] = slab[:, :].rearrange("p (a w) -> p a w", a=8)
                                slab_cur["b"] = slb
                            sl, slb = slab_cur["sl"], slab_cur["b"]
                            pu, pub = next_bank(range(6), claim=True)
                            pg, pgb = next_bank(range(6), claim=True)
                            for dk in range(8):
                                op(pe, lambda dk=dk: pe.raw.matmul(pu[:, :N], lhsT=sl[:, dk, jj * 128:(jj + 1) * 128],
                                                                   rhs=chn[:, dk, :N], start=(dk == 0), stop=(dk == 7)),
                                   reads=[slb, chnb], writes=[pub], sig=(dk == 7))
                            for dk in range(8):
                                op(pe, lambda dk=dk: pe.raw.matmul(pg[:, :N],
                                                                   lhsT=sl[:, dk, 256 + jj * 128:256 + (jj + 1) * 128],
                                                                   rhs=chn[:, dk, :N], start=(dk == 0), stop=(dk == 7)),
                                   reads=[slb, chnb], writes=[pgb], sig=(dk == 7))
                            pend[j] = (pu, pub, pg, pgb)

                        def glu_chunk(j):
                            pu, pub, pg, pgb = pend.pop(j)
                            bank_release(pub)
                            bank_release(pgb)
                            sg, sgb = next_tmp()
                            op(act, lambda: act.raw.activation(out=sg[:, :N], in_=pg[:, :N], func=AF.Sigmoid),
                               reads=[pgb], writes=[sgb])
                            op(dve, lambda: dve.raw.tensor_tensor(out=v[:, j, HALO:HALO + N], in0=pu[:, :N],
                                                                  in1=sg[:, :N], op=ALU.mult),
                               reads=[pub, sgb], writes=[vb[j]])

                        def z_group(sp_i):
                            slab, slb = slab_take(base + 8 + sp_i)
                            sl = slab[:, :].rearrange("p (a w) -> p a w", a=8)
                            for jj in range(4):
                                j = 4 * sp_i + jj
                                pz, pzb = next_bank(range(6))
                                for dk in range(8):
                                    op(pe, lambda dk=dk: pe.raw.matmul(pz[:, :N], lhsT=sl[:, dk, jj * 128:(jj + 1) * 128],
                                                                       rhs=chn[:, dk, :N], start=(dk == 0), stop=(dk == 7)),
                                       reads=[slb, chnb], writes=[pzb], sig=(dk == 7))
                                op(act, lambda j=j: act.raw.activation(out=sz[:, j, :N], in_=pz[:, :N], func=AF.Silu),
                                   reads=[pzb], writes=[szb[j]])

                        proj_chunk(0)
                        for j in range(16):
                            if j + 1 < 16:
                                proj_chunk(j + 1)
                            glu_chunk(j)
                            conv_chunk(j)
                        while pend_ev:
                            conv_evac(*pend_ev.pop(0))
                        for sp_i in range(4):
                            z_group(sp_i)
                        if my_idx + 1 < len(all_tiles):
                            prep(my_idx + 1)
                        cur.update(st_)
                        op(dve, lambda: dve.raw.tensor_scalar(out=stT[:, :N], in0=ps_s[:, :N], scalar1=1.0 / CW,
                                                              scalar2=None, op0=ALU.mult),
                           reads=[ps_sb], writes=[sttb])
                        op(dve, lambda: dve.raw.tensor_tensor(out=stB[:, :N], in0=stT[:, :N], in1=stT[:, :N], op=ALU.mult),
                           reads=[sttb], writes=[stb])
                        op(dve, lambda: dve.raw.scalar_tensor_tensor(out=stA[:, :N], in0=ps_q[:, :N], scalar=1.0 / CW,
                                                                     in1=stB[:, :N], op0=ALU.mult, op1=ALU.subtract),
                           reads=[ps_qb, stb], writes=[stb])
                        op(dve, lambda: dve.raw.tensor_scalar(out=stA[:, :N], in0=stA[:, :N], scalar1=EPS, scalar2=None,
                                                              op0=ALU.add), reads=[stb], writes=[stb])
                        op(act, lambda: act.raw.activation(out=stA[:, :N], in_=stA[:, :N], func=AF.Ln),
                           reads=[stb], writes=[stb])
                        op(act, lambda: act.raw.activation(out=stA[:, :N], in_=stA[:, :N], func=AF.Exp, scale=-0.5),
                           reads=[stb], writes=[stb])
                        op(dve, lambda: dve.raw.scalar_tensor_tensor(out=stB[:, :N], in0=stT[:, :N], scalar=-1.0,
                                                                     in1=stA[:, :N], op0=ALU.mult, op1=ALU.mult),
                           reads=[sttb, stb], writes=[stb])
                        pend_g = None
                        for j in range(16):
                            t1, t1b = next_tmp()
                            e1 = pool if j % 4 == 3 else dve
                            op(e1, lambda j=j, t1=t1, e1=e1: e1.raw.tensor_tensor(out=t1[:, :N], in0=co[:, j, :N],
                                                                                 in1=stA[:, :N], op=ALU.mult),
                               reads=[cob[j], stb], writes=[t1b])
                            op(e1, lambda t1=t1, e1=e1: e1.raw.tensor_tensor(out=t1[:, :N], in0=t1[:, :N], in1=stB[:, :N],
                                                                            op=ALU.add),
                               reads=[t1b, stb], writes=[t1b])
                            op(act, lambda j=j, t1=t1: act.raw.activation(out=t1[:, :N], in_=t1[:, :N], func=AF.Silu,
                                                                          scale=cvs("lng", lc * 16 + j, 1),
                                                                          bias=cvs("lnb", lc * 16 + j, 1)),
                               reads=[t1b, cvb], writes=[t1b])
                            if pend_g is not None:
                                pj, pt1, pt1b = pend_g
                                op(dve, lambda pj=pj, pt1=pt1: dve.raw.tensor_tensor(out=sz[:, pj, :N], in0=pt1[:, :N],
                                                                                     in1=sz[:, pj, :N], op=ALU.mult),
                                   reads=[pt1b, szb[pj]], writes=[szb[pj]])
                            pend_g = (j, t1, t1b)
                        pj, pt1, pt1b = pend_g
                        op(dve, lambda: dve.raw.tensor_tensor(out=sz[:, pj, :N], in0=pt1[:, :N], in1=sz[:, pj, :N],
                                                              op=ALU.mult),
                           reads=[pt1b, szb[pj]], writes=[szb[pj]])
                        for sp_i in range(4):
                            slab, slb = slab_take(base + 12 + sp_i)
                            sl = slab[:, :].rearrange("p (a w) -> p a w", a=16)
                            for sb in range(nsb):
                                po, pob = next_bank(range(6))
                                for ck in range(16):
                                    op(pe, lambda ck=ck, sb=sb: pe.raw.matmul(po[:rows, 0:256],
                                                                              lhsT=sz[:, ck, sb * 128:sb * 128 + rows],
                                                                              rhs=sl[:, ck, :], start=(ck == 0),
                                                                              stop=(ck == 15)),
                                       reads=[slb, szb[ck]], writes=[pob], sig=(ck == 15))
                                dsl = slice(256 * sp_i, 256 * sp_i + 256)
                                op(dve, lambda sb=sb, dsl=dsl: dve.raw.tensor_tensor(out=chT[:rows, sb, dsl],
                                                                                     in0=chT[:rows, sb, dsl],
                                                                                     in1=po[:rows, 0:256], op=ALU.add),
                                   reads=[pob, chTb], writes=[chTb])
                        if last:
                            if ti > 0:
                                final_out(s, ti, chT, chTb, foch)
                        else:
                            store_h(s, ti)
                cur.update(sets[0])
                k.barrier()

        def attn_layer(layer):
            la = layer // 2
            last = layer == depth - 1
            base = attn_base(la)
            lam_init = 0.8 - 0.6 * math.exp(-0.3 * layer)
            with ExitStack() as ls:
                kT = sbuf(f"at_kT_L{layer}", [128, NH, L], BF16, ls)
                kTb = Buf("kT")
                qT = sbuf(f"at_qT_L{layer}", [128, NH, TS], BF16, ls)
                qTb = Buf("qT")
                szg = sbuf(f"at_szg_L{layer}", [128, 4, D], BF16, ls)
                szgb = Buf("szg")
                vbuf = [sbuf(f"at_vb{i}_L{layer}", [128, NKB, 129], BF16, ls) for i in range(2)]
                vbufb = [Buf(f"vb{i}") for i in range(2)]
                vbch = [k.chan(f"vb{i}_{layer}") for i in range(2)]
                vst = [sbuf(f"at_vst{i}_L{layer}", [128, NH, 129], BF16, ls) for i in range(4)]
                vstb = [Buf(f"vst{i}") for i in range(4)]
                vsch = [k.chan("vst%d_%d" % (i, layer)) for i in range(4)]
                vdb = [Buf(f"vdram{i}") for i in range(NKB)]
                pTs = [sbuf(f"at_pT{i}_L{layer}", [128, 2, TS], BF16, ls) for i in range(3)]
                pTb = [Buf(f"pT{i}") for i in range(3)]
                accS = sbuf(f"at_accS_L{layer}", [128, 3, 387], F32, ls)
                accSb = Buf("accS")
                neghalf = sbuf(f"at_nh_L{layer}", [128, 4], F32, ls)
                nhb = Buf("neghalf")
                oo = [sbuf(f"at_oo{i}_L{layer}", [128, 128], F32, ls) for i in range(4)]
                oob = [Buf(f"oo{i}") for i in range(4)]
                tt = sbuf(f"at_tt_L{layer}", [128, 128], F32, ls)
                ttb = Buf("tt")
                szf = [sbuf(f"at_szf{i}_L{layer}", [128, TS], F32, ls) for i in range(2)]
                szfb = [Buf(f"szf{i}") for i in range(2)]
                gsub = sbuf(f"at_gsub_L{layer}", [128, D], F32, ls)
                gsubb = Buf("gsub")
                lam_t = sbuf(f"at_lam_L{layer}", [128, 8], F32, ls)
                lamb = Buf("lam")
                hnT_alt = sbuf(f"at_hnT2_L{layer}", [128, 8, TS], BF16, ls)
                hn_bufs = [(hnT, hnTb), (hnT_alt, Buf("hnT_alt"))]
                hn_ch = [k.chan(f"hnl{i}_{layer}") for i in range(2)]
                hnd_st = [k.chan(f"hnst{i}_{layer}") for i in range(2)]
                hndb = [Buf(f"hnd{i}") for i in range(len(tiles))]
                hs = sbuf(f"at_hs_L{layer}", [128, 32], F32, ls)
                hsb = Buf("hs")
                fo = fob = foch = None
                if last:
                    fo, fob = hT, hTb
                    foch = k.chan("fo_a%d" % layer)

                op(pool, lambda: pool.raw.memset(neghalf[:, :], -0.5), writes=[nhb])
                for i in range(4):
                    op(pool, lambda i=i: pool.raw.memset(vst[i][:, :, 128:129], 1.0), writes=[vstb[i]])
                lv = lambda i: cbs("lamv", (la * 4 + i) * 64, 64)
                op(dve, lambda: dve.raw.tensor_tensor(out=tt[:, 0:64], in0=lv(0), in1=lv(1), op=ALU.mult),
                   reads=[cbb], writes=[ttb])
                op(dve, lambda: dve.raw.tensor_tensor(out=tt[:, 64:128], in0=lv(2), in1=lv(3), op=ALU.mult),
                   reads=[cbb], writes=[ttb])
                op(dve, lambda: dve.raw.tensor_reduce(out=lam_t[:, 0:2], in_=tt[:, :].rearrange("p (a b) -> p a b", a=2),
                                                      axis=AX.X, op=ALU.add),
                   reads=[ttb], writes=[lamb])
                op(act, lambda: act.raw.activation(out=lam_t[:, 2:4], in_=lam_t[:, 0:2], func=AF.Exp),
                   reads=[lamb], writes=[lamb])
                op(dve, lambda: dve.raw.tensor_tensor(out=lam_t[:, 4:5], in0=lam_t[:, 2:3], in1=lam_t[:, 3:4],
                                                      op=ALU.subtract), reads=[lamb], writes=[lamb])
                op(dve, lambda: dve.raw.tensor_scalar(out=lam_t[:, 5:6], in0=lam_t[:, 4:5], scalar1=lam_init,
                                                      scalar2=None, op0=ALU.add), reads=[lamb], writes=[lamb])
                lam_ap = lam_t[:, 5:6]
                gsch = k.chan(f"gsub_{layer}")
                op(sp, lambda: sp.raw.dma_start(out=gsub[:, :], in_=sg_d[:, la * D:(la + 1) * D]),
                   writes=[gsubb], chan=gsch)
                op(dve, lambda: dve.raw.tensor_scalar(out=gsub[:, :], in0=gsub[:, :],
                                                      scalar1=1.0 - lam_init, scalar2=None, op0=ALU.mult),
                   reads=[gsubb], writes=[gsubb])

                prr = [0]
                accA = [(banks[2 + i], bankb[2 + i]) for i in range(3)]
                accB = [(banks[5 + i], bankb[5 + i]) for i in range(3)]

                def acc_region(aset, m, sb):
                    idx = m * 4 + sb
                    bk, bb = aset[idx // 3]
                    c = (idx % 3) * 129
                    return bk, bb, c

                for s in range(n_seq):
                    def prep1(ti_, par_):
                        cur.update({"hnT": hn_bufs[par_][0], "hnTb": hn_bufs[par_][1]})
                        load_h(layer, s, ti_)
                        norm_T(layer, ti_)
                        n_ = tiles[ti_][1]
                        op(sp, lambda: sp.raw.dma_start(out=hnd[ti_, :, :, 0:n_], in_=hn_bufs[par_][0][:, :, 0:n_]),
                           reads=[hn_bufs[par_][1]], writes=[hndb[ti_]], chan=hnd_st[par_])

                    prep1(0, 0)
                    for ti in range(len(tiles)):
                        t0, N, nsb, rows = tiles[ti]
                        par1 = ti % 2
                        if ti + 1 < len(tiles):
                            prep1(ti + 1, 1 - par1)
                        khn, khnb = hn_bufs[par1]
                        for half in range(2):
                            slab, slb = slab_take(base + 2 + half)
                            sl = slab[:, :].rearrange("p (a w) -> p a w", a=8)
                            for hh in range(4):
                                h = 4 * half + hh
                                pk, pkb = next_bank()
                                for dk in range(8):
                                    op(pe, lambda dk=dk: pe.raw.matmul(pk[:, :N], lhsT=sl[:, dk, hh * 128:(hh + 1) * 128],
                                                                       rhs=khn[:, dk, :N], start=(dk == 0), stop=(dk == 7)),
                                       reads=[slb, khnb], writes=[pkb], sig=(dk == 7))
                                if h % 2 == 0:
                                    op(act, lambda h=h: act.raw.activation(out=kT[:, h, t0:t0 + N], in_=pk[:, :N],
                                                                           func=AF.Copy),
                                       reads=[pkb], writes=[kTb])
                                else:
                                    op(dve, lambda h=h: dve.raw.tensor_copy(out=kT[:, h, t0:t0 + N], in_=pk[:, :N]),
                                       reads=[pkb], writes=[kTb])
                        for half in range(2):
                            slab, slb = slab_take(base + 4 + half)
                            sl = slab[:, :].rearrange("p (a w) -> p a w", a=8)
                            for sb in range(nsb):
                                pv, pvb = next_bank()
                                for dk in range(8):
                                    op(pe, lambda dk=dk, sb=sb: pe.raw.matmul(pv[:rows, :],
                                                                              lhsT=khn[:, dk, sb * 128:sb * 128 + rows],
                                                                              rhs=sl[:, dk, :], start=(dk == 0),
                                                                              stop=(dk == 7)),
                                       reads=[slb, khnb], writes=[pvb], sig=(dk == 7))
                                kb = 0 if ti == 0 else 1 + 4 * (ti - 1) + sb
                                vi = sb
                                src = pv[:rows, :].rearrange("p (h e) -> p h e", h=4)
                                e = act if sb % 2 == 0 else dve
                                if e is act:
                                    op(act, lambda vi=vi, src=src: act.raw.activation(
                                        out=vst[vi][:rows, 4 * half:4 * half + 4, 0:128], in_=src, func=AF.Copy),
                                       reads=[pvb], writes=[vstb[vi]])
                                else:
                                    op(dve, lambda vi=vi, src=src: dve.raw.tensor_copy(
                                        out=vst[vi][:rows, 4 * half:4 * half + 4, 0:128], in_=src),
                                       reads=[pvb], writes=[vstb[vi]])
                                if half == 1:
                                    op(sp, lambda kb=kb, vi=vi: sp.raw.dma_start(out=vdram[kb, :rows, :, :],
                                                                                 in_=vst[vi][:rows, :, :]),
                                       reads=[vstb[vi]], writes=[vdb[kb]], chan=vsch[vi])
                    cur.update({"hnT": hn_bufs[0][0], "hnTb": hn_bufs[0][1]})
                    p2_tiles = [ti for ti in range(len(tiles)) if not (last and ti == 0)]

                    def load_hn(ti, par):
                        n_ = tiles[ti][1]
                        op(sp, lambda: sp.raw.dma_start(out=hn_bufs[par][0][:, :, 0:n_], in_=hnd[ti, :, :, 0:n_]),
                           reads=[hndb[ti]], writes=[hn_bufs[par][1]], chan=hn_ch[par])

                    load_hn(p2_tiles[0], 0)
                    for pidx_, ti in enumerate(p2_tiles):
                        t0, N, nsb, rows = tiles[ti]
                        par = pidx_ % 2
                        chn, chnb = hn_bufs[par]
                        if pidx_ + 1 < len(p2_tiles):
                            load_hn(p2_tiles[pidx_ + 1], 1 - par)
                        load_h(layer, s, ti)
                        for half in range(2):
                            slab, slb = slab_take(base + 0 + half)
                            sl = slab[:, :].rearrange("p (a w) -> p a w", a=8)
                            for hh in range(4):
                                h = 4 * half + hh
                                pq, pqb = next_bank([0, 1, 2, 3, 7])
                                for dk in range(8):
                                    op(pe, lambda dk=dk: pe.raw.matmul(pq[:, :N], lhsT=sl[:, dk, hh * 128:(hh + 1) * 128],
                                                                       rhs=chn[:, dk, :N], start=(dk == 0), stop=(dk == 7)),
                                       reads=[slb, chnb], writes=[pqb], sig=(dk == 7))
                                op(dve, lambda h=h: dve.raw.tensor_copy(out=qT[:, h, :N], in_=pq[:, :N]),
                                   reads=[pqb], writes=[qTb])
                        for half in range(2):
                            slab, slb = slab_take(base + 6 + half)
                            sl = slab[:, :].rearrange("p (a w) -> p a w", a=8)
                            for sb in range(nsb):
                                pz, pzb = next_bank([0, 1, 2, 3, 7])
                                for dk in range(8):
                                    op(pe, lambda dk=dk, sb=sb: pe.raw.matmul(pz[:rows, :],
                                                                              lhsT=chn[:, dk, sb * 128:sb * 128 + rows],
                                                                              rhs=sl[:, dk, :], start=(dk == 0),
                                                                              stop=(dk == 7)),
                                       reads=[slb, chnb], writes=[pzb], sig=(dk == 7))
                                zi = sb % 2
                                op(act, lambda zi=zi: act.raw.activation(out=szf[zi][:rows, :], in_=pz[:rows, :],
                                                                         func=AF.Silu),
                                   reads=[pzb], writes=[szfb[zi]])
                                op(dve, lambda zi=zi, sb=sb: dve.raw.tensor_tensor(
                                    out=szg[:rows, sb, half * 512:(half + 1) * 512], in0=szf[zi][:rows, :],
                                    in1=gsub[:rows, half * 512:(half + 1) * 512], op=ALU.mult),
                                   reads=[szfb[zi], gsubb], writes=[szgb])
                        nkb = 1 if ti == 0 else 4 * ti + 1
                        first_kb = 0 if ti == 0 else 4 * (ti - 1) + 1

                        def load_v(h):
                            vi = h % 2
                            op(sp, lambda: sp.raw.dma_start(out=vbuf[vi][:NMETA, 0, :], in_=vdram[0, :NMETA, h, :]),
                               reads=vdb[0:1], writes=[vbufb[vi]], chan=vbch[vi])
                            if nkb > 1:
                                src = vdram[1:nkb, :, h, :].rearrange("k p e -> p k e")
                                op(sp, lambda: sp.raw.dma_start(out=vbuf[vi][:, 1:nkb, :], in_=src),
                                   reads=vdb[1:nkb], writes=[vbufb[vi]], chan=vbch[vi])

                        def geom(kb):
                            rows_k = NMETA if kb == 0 else 128
                            kp0 = 0 if kb == 0 else NMETA + 128 * (kb - 1)
                            diag = kb >= first_kb
                            c0 = (kb - first_kb) * 128 if (diag and ti > 0) else 0
                            return rows_k, kp0, diag, c0

                        def emit_qk(h, kb, pair):
                            rows_k, kp0, diag, c0 = geom(kb)
                            for m in range(2):
                                st = banks[2 * pair + m]
                                stb_ = bankb[2 * pair + m]
                                op(pe, lambda m=m, st=st: pe.raw.matmul(st[:rows_k, c0:N],
                                                                        lhsT=kT[64 * m:64 * m + 64, h, kp0:kp0 + rows_k],
                                                                        rhs=qT[64 * m:64 * m + 64, h, c0:N],
                                                                        start=True, stop=(not diag)),
                                   reads=[kTb, qTb], writes=[stb_], sig=(not diag))
                                if diag:
                                    w = min(128, N - c0)
                                    op(pe, lambda st=st, w=w: pe.raw.matmul(st[:rows_k, c0:c0 + w],
                                                                            lhsT=identb[:rows_k, :rows_k],
                                                                            rhs=maskb[:rows_k, :w], start=False, stop=True),
                                       reads=[constb], writes=[stb_], sig=True)

                        def emit_exp(h, kb, pair, pi):
                            rows_k, kp0, diag, c0 = geom(kb)
                            gs = min(GS[h], N)
                            pt, ptb = pTs[pi], pTb[pi]
                            for g in range((N + gs - 1) // gs):
                                a_ = max(c0, g * gs)
                                b_ = min(N, (g + 1) * gs)
                                if b_ <= a_:
                                    continue
                                if ti == 0:
                                    bias = cvs("T2", h * 34 + 33, 1)
                                elif kb == 0:
                                    n_ = 4 * (ti - 1) + (g * gs) // 128
                                    bias = cvs("T2", h * 34 + n_, 1)
                                else:
                                    delta = (kb - 1) - 4 * (ti - 1) - (g * gs) // 128
                                    bias = cvs("T1", h * 36 + delta + 32, 1)
                                op(act, lambda a_=a_, b_=b_, bias=bias: act.raw.activation(
                                    out=pt[:rows_k, :, a_:b_], in_=ps_all[:rows_k, 2 * pair:2 * pair + 2, a_:b_],
                                    func=AF.Exp, bias=bias[:rows_k, :], scale=0.125),
                                   reads=[bankb[2 * pair], bankb[2 * pair + 1], cvb], writes=[ptb])

                        def pv_plan():
                            plan = []
                            for kb in range(nkb):
                                c0_ = geom(kb)[3]
                                for m in range(2):
                                    for sb in range(nsb):
                                        if sb * 128 < c0_:
                                            continue
                                        plan.append((kb, m, sb, (m * 4 + sb) // 3))
                            first_in, last_in = {}, {}
                            for i_, (kb_, m_, sb_, bi_) in enumerate(plan):
                                first_in.setdefault(bi_, i_)
                                last_in[bi_] = i_
                            return plan, first_in, last_in

                        plan, first_in, last_in = pv_plan()
                        plan_by_kb = {}
                        for i_, (kb_, m_, sb_, bi_) in enumerate(plan):
                            plan_by_kb.setdefault(kb_, []).append((i_, m_, sb_, bi_))

                        def emit_pv(h, kb, pi):
                            rows_k = geom(kb)[0]
                            vi = h % 2
                            pt, ptb = pTs[pi], pTb[pi]
                            for (i_, m, sb, bi_) in plan_by_kb[kb]:
                                c = ((m * 4 + sb) % 3) * 129
                                bk, bb = banks[4 + bi_], bankb[4 + bi_]
                                op(pe, lambda m=m, sb=sb, bk=bk, c=c, i_=i_, bi_=bi_: pe.raw.matmul(
                                    bk[:rows, c:c + 129], lhsT=pt[:rows_k, m, sb * 128:sb * 128 + rows],
                                    rhs=vbuf[vi][:rows_k, kb, :], start=(first_in[bi_] == i_),
                                    stop=(last_in[bi_] == i_)),
                                   reads=[ptb, vbufb[vi]], writes=[bb], sig=True)

                        def emit_post(h):
                            op(dve, lambda: dve.raw.tensor_copy(out=accS[:rows, :, :], in_=ps_all[:rows, 4:7, 0:387]),
                               reads=[bankb[4], bankb[5], bankb[6]], writes=[accSb])
                            rec = hs[:, 0:8]
                            r1l = hs[:, 8:12]
                            ssq = hs[:, 12:16]
                            rsd = hs[:, 16:20]
                            tq = hs[:, 20:24]

                            def areg(m, sb):
                                idx = m * 4 + sb
                                return accS[:rows, idx // 3, (idx % 3) * 129:(idx % 3) * 129 + 129]
                            for m in range(2):
                                for sb in range(nsb):
                                    op(dve, lambda m=m, sb=sb: dve.raw.reciprocal(
                                        out=rec[:rows, m * 4 + sb:m * 4 + sb + 1], in_=areg(m, sb)[:, 128:129]),
                                       reads=[accSb], writes=[hsb])
                            op(dve, lambda: dve.raw.tensor_scalar(out=r1l[:rows, 0:nsb], in0=rec[:rows, 4:4 + nsb],
                                                                  scalar1=lam_ap[:rows, :], scalar2=None, op0=ALU.mult),
                               reads=[hsb, lamb], writes=[hsb])
                            for sb in range(nsb):
                                op(dve, lambda sb=sb: dve.raw.tensor_scalar(
                                    out=tt[:rows, :], in0=areg(1, sb)[:, 0:128], scalar1=r1l[:rows, sb:sb + 1],
                                    scalar2=None, op0=ALU.mult),
                                   reads=[accSb, hsb], writes=[ttb])
                                op(dve, lambda sb=sb: dve.raw.scalar_tensor_tensor(
                                    out=oo[sb][:rows, :], in0=areg(0, sb)[:, 0:128], scalar=rec[:rows, sb:sb + 1],
                                    in1=tt[:rows, :], op0=ALU.mult, op1=ALU.subtract),
                                   reads=[accSb, hsb, ttb], writes=[oob[sb]])
                                op(dve, lambda sb=sb: dve.raw.scalar_tensor_tensor(
                                    out=tt[:rows, :], in0=oo[sb][:rows, :], scalar=1.0, in1=oo[sb][:rows, :],
                                    op0=ALU.mult, op1=ALU.mult, accum_out=ssq[:rows, sb:sb + 1]),
                                   reads=[oob[sb]], writes=[ttb, hsb])
                            op(dve, lambda: dve.raw.tensor_scalar(out=tq[:rows, 0:nsb], in0=ssq[:rows, 0:nsb],
                                                                  scalar1=1.0 / 128.0, scalar2=EPS, op0=ALU.mult,
                                                                  op1=ALU.add), reads=[hsb], writes=[hsb])
                            op(pool, lambda: pool.raw.tensor_tensor(out=rsd[:rows, 0:nsb], in0=tq[:rows, 0:nsb],
                                                                    in1=neghalf[:rows, 0:nsb], op=ALU.pow),
                               reads=[hsb, nhb], writes=[hsb])
                            for sb in range(nsb):
                                op(dve, lambda sb=sb: dve.raw.scalar_tensor_tensor(
                                    out=tokA[:rows, sb, h * 128:(h + 1) * 128], in0=oo[sb][:rows, :],
                                    scalar=rsd[:rows, sb:sb + 1], in1=szg[:rows, sb, h * 128:(h + 1) * 128],
                                    op0=ALU.mult, op1=ALU.mult),
                                   reads=[oob[sb], hsb, szgb], writes=[tokAb])

                        items = [(h, kb) for h in range(NH) for kb in range(nkb)]
                        load_v(0)
                        emit_qk(items[0][0], items[0][1], 0)
                        for i, (h, kb) in enumerate(items):
                            if kb == 0 and h + 1 < NH:
                                load_v(h + 1)
                            if i + 1 < len(items):
                                emit_qk(items[i + 1][0], items[i + 1][1], (i + 1) % 2)
                            pi = prr[0] % len(pTs)
                            prr[0] += 1
                            emit_exp(h, kb, i % 2, pi)
                            emit_pv(h, kb, pi)
                            if kb == nkb - 1:
                                emit_post(h)
                        transpose_to(chn, chnb, None, N, nsb, rows)
                        for half in range(2):
                            slab, slb = slab_take(base + 8 + half)
                            sl = slab[:, :].rearrange("p (a w) -> p a w", a=8)
                            for sb in range(nsb):
                                po, pob = next_bank([0, 1, 2, 3, 7])
                                for ck in range(8):
                                    op(pe, lambda ck=ck, sb=sb: pe.raw.matmul(po[:rows, :],
                                                                              lhsT=chn[:, ck, sb * 128:sb * 128 + rows],
                                                                              rhs=sl[:, ck, :], start=(ck == 0),
                                                                              stop=(ck == 7)),
                                       reads=[slb, chnb], writes=[pob], sig=(ck == 7))
                                dsl = slice(512 * half, 512 * half + 512)
                                op(dve, lambda sb=sb, dsl=dsl: dve.raw.tensor_tensor(out=hT[:rows, sb, dsl],
                                                                                     in0=hT[:rows, sb, dsl],
                                                                                     in1=po[:rows, :], op=ALU.add),
                                   reads=[pob, hTb], writes=[hTb])
                        if last:
                            final_out(s, ti, fo, fob, foch)
                        else:
                            store_h(s, ti)
                k.barrier()

        for layer in range(depth):
            if layer % 2 == 0:
                conv_layer(layer)
            else:
                attn_layer(layer)
        assert sl_state["taken"] == len(slab_seq), (sl_state, len(slab_seq))
        k.barrier()
    return nc


_NC_CACHE = {}


def kernel(x, meta_tokens, norm_gain, final_norm_gain,
           conv_w_in, conv_dw_kernel, conv_dw_bias, conv_ln_gain, conv_ln_bias, conv_w_out,
           attn_w_in, attn_lambda_q1, attn_lambda_k1, attn_lambda_q2, attn_lambda_k2,
           attn_subln_gain, attn_w_out):
    n_cores = 8
    x = np.asarray(x, np.float32)
    B, SEQ, _ = x.shape
    n_seq = B // n_cores
    n_xt = SEQ // TS
    inp = dict(norm_gain=norm_gain, final_norm_gain=final_norm_gain, conv_dw_kernel=conv_dw_kernel,
               conv_dw_bias=conv_dw_bias, conv_ln_gain=conv_ln_gain, conv_ln_bias=conv_ln_bias,
               attn_lambda_q1=attn_lambda_q1, attn_lambda_k1=attn_lambda_k1, attn_lambda_q2=attn_lambda_q2,
               attn_lambda_k2=attn_lambda_k2, attn_subln_gain=attn_subln_gain)
    cv, cb = host_consts(inp)
    depth = int(np.asarray(norm_gain).shape[0])
    key = (n_seq, n_xt, depth)
    if key not in _NC_CACHE:
        _NC_CACHE[key] = build_program(n_seq, n_xt, depth)
    nc = _NC_CACHE[key]
    common = {
        "meta": np.ascontiguousarray(np.asarray(meta_tokens, np.float32)),
        "cv": cv, "cb": cb,
        "cwi": np.ascontiguousarray(np.asarray(conv_w_in, np.float32)),
        "cwo": np.ascontiguousarray(np.asarray(conv_w_out, np.float32)),
        "awi": np.ascontiguousarray(np.asarray(attn_w_in, np.float32)),
        "awo": np.ascontiguousarray(np.asarray(attn_w_out, np.float32)),
        "tz": host_toeplitz(conv_dw_kernel),
        "sgb": host_subln(inp),
    }
    in_maps = []
    for c in range(n_cores):
        d = dict(common)
        d["x"] = np.ascontiguousarray(x[c * n_seq:(c + 1) * n_seq])
        in_maps.append(d)
    res = run_bass_kernel_spmd(nc, in_maps, core_ids=list(range(n_cores)))
    out = np.concatenate([np.asarray(r["out"], np.float32) for r in res.results], axis=0)
    return out
```

```python
import math
from contextlib import ExitStack

import numpy as np
import concourse.bass as bass
import concourse.mybir as mybir
from concourse.bass_utils import run_bass_kernel_spmd

F32 = mybir.dt.float32
BF16 = mybir.dt.bfloat16
AF = mybir.ActivationFunctionType
ALU = mybir.AluOpType
AX = mybir.AxisListType

D = 1024
NMETA = 16
TS = 512
EPS = 1e-6
CW = 2048
KC = 31
NH = 8
MASKNEG = -30000.0
HALO = 32
SLAB = 4096
import os
PRODMODE = int(os.environ.get("PRODMODE", "0"))
DBG = int(os.environ.get("DBG", "0"))
GS = [128, 256, 512, 512, 512, 512, 512, 512]


class Eng:
    def __init__(self, nc, name, raw, selfsync):
        self.name = name
        self.raw = raw
        self.sem = nc.alloc_semaphore(name="sem_" + name)
        self.count = 0
        self.known = {}
        self.selfsync = selfsync


class Chan:
    def __init__(self, nc, name):
        self.name = name
        self.sem = nc.alloc_semaphore(name="ch_" + name)
        self.count = 0


class Buf:
    __slots__ = ("name", "w", "r")

    def __init__(self, name):
        self.name = name
        self.w = None
        self.r = {}


class K:
    def __init__(self, nc):
        self.nc = nc
        self.pe = Eng(nc, "pe", nc.tensor, False)
        self.act = Eng(nc, "act", nc.scalar, True)
        self.dve = Eng(nc, "dve", nc.vector, True)
        self.pool = Eng(nc, "pool", nc.gpsimd, True)
        self.sp = Eng(nc, "sp", nc.sync, False)
        self.engs = [self.pe, self.act, self.dve, self.pool, self.sp]
        self.chans = []

    def chan(self, name):
        c = Chan(self.nc, name)
        self.chans.append(c)
        return c

    def op(self, eng, fn, reads=(), writes=(), sig=True, chan=None, noself=False):
        need = {}
        for b in reads:
            if b.w is not None and need.get(b.w[0], 0) < b.w[1]:
                need[b.w[0]] = b.w[1]
        for b in writes:
            if b.w is not None and need.get(b.w[0], 0) < b.w[1]:
                need[b.w[0]] = b.w[1]
            for s, v in b.r.items():
                if need.get(s, 0) < v:
                    need[s] = v
        for src, val in need.items():
            if src is eng:
                if noself or not eng.selfsync or val > eng.count:
                    continue
            if eng.known.get(src, 0) >= val:
                continue
            eng.raw.wait_ge(src.sem, val)
            eng.known[src] = val
        ins = fn()
        if chan is not None:
            chan.count += 16
            ins.then_inc(chan.sem, 16)
            me = (chan, chan.count)
        elif sig:
            eng.count += 1
            ins.then_inc(eng.sem, 1)
            me = (eng, eng.count)
        else:
            me = (eng, eng.count + 1)
        for b in reads:
            if b.r.get(me[0], 0) < me[1]:
                b.r[me[0]] = me[1]
        for b in writes:
            b.w = me
            b.r = {}
        return ins

    def barrier(self):
        srcs = [(e, e.count) for e in self.engs if e.count > 0] + [(c, c.count) for c in self.chans if c.count > 0]
        for e in self.engs:
            for s, v in srcs:
                if s is e:
                    continue
                if e.known.get(s, 0) >= v:
                    continue
                e.raw.wait_ge(s.sem, v)
                e.known[s] = v


def cv_layout():
    off = {}
    o = 0
    for name, n in [("gT", 4 * 8), ("dwk", 2 * 16 * KC), ("dwb", 32), ("lng", 32), ("lnb", 32),
                    ("T1", NH * 36), ("T2", NH * 34), ("ident", 128), ("mask", 128)]:
        off[name] = (o, n)
        o += n
    return off, o


def cb_layout():
    off = {}
    o = 0
    for name, n in [("fg", D), ("lamv", 2 * 4 * 64)]:
        off[name] = (o, n)
        o += n
    return off, o


def host_consts(inp):
    off, ncv = cv_layout()
    cv = np.zeros((128, ncv), np.float32)
    p = np.arange(128)

    def put(name, arr):
        o, n = off[name]
        cv[:, o:o + n] = np.asarray(arr, np.float32).reshape(128, n)

    ng = np.asarray(inp["norm_gain"], np.float32)
    put("gT", ng.reshape(4, 8, 128).transpose(2, 0, 1))
    dk = np.asarray(inp["conv_dw_kernel"], np.float32)
    put("dwk", dk.reshape(2, KC, 16, 128).transpose(3, 0, 2, 1))
    for nm, key in [("dwb", "conv_dw_bias"), ("lng", "conv_ln_gain"), ("lnb", "conv_ln_bias")]:
        a = np.asarray(inp[key], np.float32)
        put(nm, a.reshape(2, 16, 128).transpose(2, 0, 1))
    slopes = 2.0 ** (-(np.arange(NH) + 1.0))
    d1 = np.arange(-32, 4)
    t1 = slopes[None, :, None] * (p[:, None, None] + 128.0 * d1[None, None, :])
    put("T1", t1)
    n2 = np.arange(34)
    t2 = slopes[None, :, None] * (p[:, None, None] - 16.0 - 128.0 * n2[None, None, :])
    t2[:, :, 33] = slopes[None, :] * p[:, None]
    put("T2", t2)
    put("ident", np.eye(128, dtype=np.float32))
    m = np.where(p[None, :] < p[:, None], MASKNEG, 0.0)
    put("mask", m)

    offb, ncb = cb_layout()
    cb = np.zeros((1, ncb), np.float32)
    o, n = offb["fg"]
    cb[0, o:o + n] = np.asarray(inp["final_norm_gain"], np.float32)
    o, n = offb["lamv"]
    lv = np.stack([np.asarray(inp[k], np.float32) for k in
                   ("attn_lambda_q1", "attn_lambda_k1", "attn_lambda_q2", "attn_lambda_k2")], axis=1)
    cb[0, o:o + n] = lv.reshape(-1)
    cb = np.ascontiguousarray(np.broadcast_to(cb, (128, ncb)))
    return cv, cb


def host_subln(inp):
    sg = np.asarray(inp["attn_subln_gain"], np.float32)
    row = np.tile(sg[:, None, :], (1, NH, 1)).reshape(1, -1)
    return np.ascontiguousarray(np.broadcast_to(row, (128, row.shape[1])))


def host_toeplitz(dw):
    dw = np.asarray(dw, np.float32)
    n = dw.shape[0]
    wc = dw.transpose(0, 2, 1)
    j = np.arange(32)[:, None]
    i = np.arange(32)[None, :]
    d = i - j
    k0 = 30 - d
    m0 = (d >= 0) & (d <= 30)
    k1 = -d - 2
    m1 = k1 >= 0
    T0 = np.where(m0[None, None], wc[:, :, np.clip(k0, 0, 30)], np.float32(0))
    T1 = np.where(m1[None, None], wc[:, :, np.clip(k1, 0, 30)], np.float32(0))
    T = np.stack([T0, T1], axis=2).astype(np.float32)
    T = T.reshape(n, 16, 4, 32, 2, 32, 32).transpose(0, 1, 2, 5, 3, 4, 6)
    return np.ascontiguousarray(T.reshape(n, 16, 128, 2048))


def build_program(n_seq=2, n_xt=8, depth=4, stop_after=None):
    SEQ = n_xt * TS
    L = NMETA + SEQ
    NKB = 1 + 4 * n_xt
    n_conv = (depth + 1) // 2
    n_attn = depth // 2
    nc = bass.Bass("TRN2", target_bir_lowering=False)
    offv, ncv = cv_layout()
    offb, ncb = cb_layout()

    x_d = nc.dram_tensor("x", [n_seq, SEQ, D], F32, kind="ExternalInput").ap()
    meta_d = nc.dram_tensor("meta", [NMETA, D], F32, kind="ExternalInput").ap()
    cv_d = nc.dram_tensor("cv", [128, ncv], F32, kind="ExternalInput").ap()
    cb_d = nc.dram_tensor("cb", [128, ncb], F32, kind="ExternalInput").ap()
    sg_d = nc.dram_tensor("sgb", [128, 2 * D], F32, kind="ExternalInput").ap()
    cwi_d = nc.dram_tensor("cwi", [n_conv, D, 3 * CW], F32, kind="ExternalInput").ap()
    cwo_d = nc.dram_tensor("cwo", [n_conv, CW, D], F32, kind="ExternalInput").ap()
    awi_d = nc.dram_tensor("awi", [max(n_attn, 1), D, 4 * D], F32, kind="ExternalInput").ap()
    awo_d = nc.dram_tensor("awo", [max(n_attn, 1), D, D], F32, kind="ExternalInput").ap()
    out_d = nc.dram_tensor("out", [n_seq, SEQ, D], F32, kind="ExternalOutput").ap()
    hbuf = nc.dram_tensor("hbuf", [n_seq, L, D], F32, kind="Internal").ap()
    NSLAB = 16 * n_conv + 10 * n_attn
    wscr = nc.dram_tensor("wscr", [NSLAB, 128, SLAB], BF16, kind="Internal").ap()
    vdram = nc.dram_tensor("vdram", [NKB, 128, NH, 129], BF16, kind="Internal").ap()
    hnd = nc.dram_tensor("hnd", [1 + n_xt, 128, 8, TS], BF16, kind="Internal").ap()
    tz_d = nc.dram_tensor("tz", [n_conv, 16, 128, 2048], F32, kind="ExternalInput").ap()
    wtoep = nc.dram_tensor("wtoep", [n_conv * 16, 128, 8192], BF16, kind="Internal").ap()

    k = K(nc)
    pe, act, dve, pool, sp = k.pe, k.act, k.dve, k.pool, k.sp
    op = k.op

    tiles = [(0, NMETA, 1, NMETA)] + [(NMETA + TS * i, TS, 4, 128) for i in range(n_xt)]

    def slab_ids_conv(lc):
        return [16 * lc + i for i in range(16)]

    def attn_base(la):
        return 16 * n_conv + 10 * la

    slab_seq = []
    for layer in range(depth):
        last = layer == depth - 1
        for s in range(n_seq):
            if layer % 2 == 0:
                for ti in range(len(tiles)):
                    slab_seq += slab_ids_conv(layer // 2)
            else:
                b = attn_base(layer // 2)
                for ti in range(len(tiles)):
                    slab_seq += [b + 2, b + 3, b + 4, b + 5]
                for ti in range(len(tiles)):
                    if last and ti == 0:
                        continue
                    slab_seq += [b + 0, b + 1, b + 6, b + 7, b + 8, b + 9]

    toep_seq = []
    for layer in range(0, depth, 2):
        for s_ in range(n_seq):
            for ti in range(1, len(tiles)):
                toep_seq += [16 * (layer // 2) + j for j in range(16)]

    with ExitStack() as es:
        def sbuf(name, shape, dt, stack=None):
            return (stack or es).enter_context(nc.sbuf_tensor(name, shape, dt))

        ps_all = es.enter_context(nc.psum_tensor("ps_all", [128, 8, 512], F32))
        banks = [ps_all[:, i, :] for i in range(8)]
        bankb = [Buf(f"bank{i}") for i in range(8)]
        cv = sbuf("cv_sb", [128, ncv], F32)
        cvb = Buf("cv")
        cbt = sbuf("cb_sb", [128, ncb], F32)
        cbb = Buf("cb")
        identb = sbuf("identb", [128, 128], BF16)
        maskb = sbuf("maskb", [128, 128], BF16)
        constb = Buf("constb")
        hT = sbuf("hT", [128, 4, D], F32)
        hTb = Buf("hT")
        tokA = sbuf("tokA", [128, 4, D], BF16)
        tokAb = Buf("tokA")
        hnT = sbuf("hnT", [128, 8, TS], BF16)
        hnTb = Buf("hnT")
        junk = sbuf("junk", [128, D], BF16)
        junkb = Buf("junk")
        small = sbuf("small", [128, 64], F32)
        ss_b, rs_b, tmp_b = Buf("ss"), Buf("rs"), Buf("tmp")
        NSLOT = 2
        slabs = [sbuf(f"slab{i}", [128, SLAB], BF16) for i in range(NSLOT)]
        slabb = [Buf(f"slab{i}") for i in range(NSLOT)]
        slabch = [k.chan(f"slab{i}") for i in range(NSLOT)]
        ch_h = k.chan("hload")
        ch_st = k.chan("hstore")
        ch_c = k.chan("const")
        ch_c2 = k.chan("const2")
        wscrb = [Buf(f"wscr{i}") for i in range(NSLAB)]
        wtoepb = [Buf(f"wtoep{i}") for i in range(n_conv * 16)]
        hbufb = [[Buf(f"hbuf{s}_{t}") for t in range(len(tiles))] for s in range(n_seq)]
        outb = Buf("out")

        def cvs(name, i0=0, n=None):
            o, tot = offv[name]
            n = tot if n is None else n
            return cv[:, o + i0:o + i0 + n]

        def cbs(name, i0=0, n=None):
            o, tot = offb[name]
            n = tot if n is None else n
            return cbt[:, o + i0:o + i0 + n]

        bank_rr = [0]

        bank_live = set()

        def next_bank(pool_ids=range(8), claim=False):
            ids = list(pool_ids)
            i = ids[bank_rr[0] % len(ids)]
            bank_rr[0] += 1
            assert i not in bank_live, ("PSUM bank still live", i)
            if claim:
                bank_live.add(i)
            return banks[i], bankb[i]

        def bank_release(bb):
            bank_live.discard(bankb.index(bb))

        op(sp, lambda: sp.raw.dma_start(out=cv[:], in_=cv_d[:, :]), writes=[cvb], chan=ch_c)
        op(sp, lambda: sp.raw.dma_start(out=cbt[:], in_=cb_d[:, :]), writes=[cbb], chan=ch_c2)
        op(dve, lambda: dve.raw.tensor_copy(out=identb[:], in_=cvs("ident")), reads=[cvb], writes=[constb])
        op(dve, lambda: dve.raw.tensor_copy(out=maskb[:], in_=cvs("mask")), reads=[cvb], writes=[constb])

        with ExitStack() as ps_:
            NST = 3
            stg = [sbuf(f"stg{i}", [128, SLAB], F32, ps_) for i in range(NST)]
            stgb = [Buf(f"stg{i}") for i in range(NST)]
            stgch = [k.chan(f"stg{i}") for i in range(NST)]
            sto = [sbuf(f"sto{i}", [128, SLAB], BF16, ps_) for i in range(2)]
            stob = [Buf(f"sto{i}") for i in range(2)]
            stoch = [k.chan(f"sto{i}") for i in range(2)]
            tf = [sbuf(f"tf{i}", [128, 64, 128], BF16, ps_) for i in range(2)]
            tfb = [Buf(f"tf{i}") for i in range(2)]
            tfch = [k.chan(f"tf{i}") for i in range(2)]
            for i in range(2):
                op(pool, lambda i=i: pool.raw.memset(tf[i][:, :, :], 0.0), writes=[tfb[i]])
            units = []

            for lc in range(n_conv):
                wi = cwi_d[lc].rearrange("(a p) c -> p a c", p=128)
                wo = cwo_d[lc].rearrange("(a p) c -> p a c", p=128)
                for s_ in range(8):
                    units.append(("w", 16 * lc + s_, [(wi[:, :, 256 * s_:256 * s_ + 256], 8, 0, 256),
                                                      (wi[:, :, CW + 256 * s_:CW + 256 * s_ + 256], 8, 256, 256)]))
                for s_ in range(4):
                    units.append(("w", 16 * lc + 8 + s_, [(wi[:, :, 2 * CW + 512 * s_:2 * CW + 512 * s_ + 512], 8, 0, 512)]))
                for s_ in range(4):
                    units.append(("w", 16 * lc + 12 + s_, [(wo[:, :, 256 * s_:256 * s_ + 256], 16, 0, 256)]))
                for cc in range(16):
                    units.append(("t", lc * 16 + cc, (lc, cc)))
            for la in range(n_attn):
                wi = awi_d[la].rearrange("(a p) c -> p a c", p=128)
                wo = awo_d[la].rearrange("(a p) c -> p a c", p=128)
                b = attn_base(la)
                for s_ in range(8):
                    units.append(("w", b + s_, [(wi[:, :, 512 * s_:512 * s_ + 512], 8, 0, 512)]))
                for s_ in range(2):
                    units.append(("w", b + 8 + s_, [(wo[:, :, 512 * s_:512 * s_ + 512], 8, 0, 512)]))

            def u_load(n):
                kind, sid, info = units[n]
                i = n % NST
                if kind == "w":
                    for (src, A, w0, W) in info:
                        dst = stg[i][:, :].rearrange("p (a w) -> p a w", a=A)[:, :, w0:w0 + W]
                        op(sp, lambda dst=dst, src=src: sp.raw.dma_start(out=dst, in_=src),
                           writes=[stgb[i]], chan=stgch[i])
                else:
                    lc, cc = info
                    op(sp, lambda: sp.raw.dma_start(out=stg[i][:, 0:2048], in_=tz_d[lc, cc, :, :]),
                       writes=[stgb[i]], chan=stgch[i])

            wcnt, tcnt = [0], [0]

            def u_cast_store(n):
                kind, sid, info = units[n]
                i = n % NST
                if kind == "w":
                    o = wcnt[0] % 2
                    wcnt[0] += 1
                    if n % 2 == 0:
                        op(act, lambda: act.raw.activation(out=sto[o][:], in_=stg[i][:], func=AF.Copy),
                           reads=[stgb[i]], writes=[stob[o]])
                    else:
                        op(dve, lambda: dve.raw.tensor_copy(out=sto[o][:], in_=stg[i][:]),
                           reads=[stgb[i]], writes=[stob[o]])
                    op(sp, lambda: sp.raw.dma_start(out=wscr[sid, :, :], in_=sto[o][:]),
                       reads=[stob[o]], writes=[wscrb[sid]], chan=stoch[o])
                else:
                    o = tcnt[0] % 2
                    tcnt[0] += 1
                    for q in range(4):
                        src = stg[i][32 * q:32 * q + 32, 0:2048].rearrange("p (a i) -> p a i", i=32)
                        dst = tf[o][32 * q:32 * q + 32, :, 32 * q:32 * q + 32]
                        if q % 2 == 0:
                            op(dve, lambda src=src, dst=dst: dve.raw.tensor_copy(out=dst, in_=src),
                               reads=[stgb[i]], writes=[tfb[o]])
                        else:
                            op(act, lambda src=src, dst=dst: act.raw.activation(out=dst, in_=src, func=AF.Copy),
                               reads=[stgb[i]], writes=[tfb[o]])
                    op(sp, lambda: sp.raw.dma_start(out=wtoep[sid, :, :],
                                                    in_=tf[o][:, :, :].rearrange("p a m -> p (a m)")),
                       reads=[tfb[o]], writes=[wtoepb[sid]], chan=tfch[o])

            u_load(0)
            if len(units) > 1:
                u_load(1)
            for n in range(len(units)):
                if n + 2 < len(units):
                    u_load(n + 2)
                u_cast_store(n)
            k.barrier()

        sl_state = {"issued": 0, "taken": 0}

        def slab_issue():
            i = sl_state["issued"]
            if i >= len(slab_seq):
                return
            slot = i % NSLOT
            sid = slab_seq[i]
            op(sp, lambda: sp.raw.dma_start(out=slabs[slot][:], in_=wscr[sid, :, :]),
               reads=[wscrb[sid]], writes=[slabb[slot]], chan=slabch[slot])
            sl_state["issued"] += 1

        def slab_take(expect):
            i = sl_state["taken"]
            assert slab_seq[i] == expect, (i, slab_seq[i], expect)
            while sl_state["issued"] < min(i + NSLOT, len(slab_seq)):
                slab_issue()
            sl_state["taken"] += 1
            slot = i % NSLOT
            return slabs[slot], slabb[slot]

        tp_state = {"issued": 0, "taken": 0, "slabs": None, "bufs": None, "chans": None, "limit": 0}

        def toep_issue():
            i = tp_state["issued"]
            if i >= len(toep_seq):
                return
            slot = i % 2
            tid = toep_seq[i]
            op(sp, lambda: sp.raw.dma_start(out=tp_state["slabs"][slot][:], in_=wtoep[tid, :, :]),
               reads=[wtoepb[tid]], writes=[tp_state["bufs"][slot]], chan=tp_state["chans"][slot])
            tp_state["issued"] += 1

        def toep_take(expect):
            i = tp_state["taken"]
            assert toep_seq[i] == expect, (i, toep_seq[i], expect)
            while tp_state["issued"] < min(i + 2, tp_state["limit"]):
                toep_issue()
            tp_state["taken"] += 1
            return tp_state["slabs"][i % 2], tp_state["bufs"][i % 2]

        cur = {"hT": hT, "hTb": hTb, "tokA": tokA, "tokAb": tokAb, "hnT": hnT, "hnTb": hnTb, "hch": ch_h, "sch": ch_st}

        def rstd_from(ss_ap, out_ap, n, rd, wr):
            t = small[:ss_ap.shape[0], 32:32 + ss_ap.shape[1]]
            op(dve, lambda: dve.raw.tensor_scalar(out=t, in0=ss_ap, scalar1=1.0 / n, scalar2=EPS,
                                                  op0=ALU.mult, op1=ALU.add), reads=[rd], writes=[tmp_b])
            op(act, lambda: act.raw.activation(out=t, in_=t, func=AF.Ln), reads=[tmp_b], writes=[tmp_b])
            op(act, lambda: act.raw.activation(out=out_ap, in_=t, func=AF.Exp, scale=-0.5),
               reads=[tmp_b], writes=[wr])

        def load_h(layer, s, ti):
            hT, hTb = cur["hT"], cur["hTb"]
            t0, N, nsb, rows = tiles[ti]
            if layer == 0:
                if ti == 0:
                    src = meta_d[:, :]
                    dst = hT[:NMETA, 0, :]
                else:
                    src = x_d[s, t0 - NMETA:t0 - NMETA + N, :].rearrange("(s p) d -> p s d", p=128)
                    dst = hT[:, :, :]
                op(pool, lambda: pool.raw.dma_start(out=dst, in_=src), writes=[hTb], chan=cur["hch"])
            else:
                if ti == 0:
                    src = hbuf[s, 0:NMETA, :]
                    dst = hT[:NMETA, 0, :]
                else:
                    src = hbuf[s, t0:t0 + N, :].rearrange("(s p) d -> p s d", p=128)
                    dst = hT[:, :, :]
                op(pool, lambda: pool.raw.dma_start(out=dst, in_=src), reads=[hbufb[s][ti]], writes=[hTb], chan=cur["hch"])

        def store_h(s, ti):
            hT, hTb = cur["hT"], cur["hTb"]
            t0, N, nsb, rows = tiles[ti]
            if ti == 0:
                dst = hbuf[s, 0:NMETA, :]
                src = hT[:NMETA, 0, :]
            else:
                dst = hbuf[s, t0:t0 + N, :].rearrange("(s p) d -> p s d", p=128)
                src = hT[:, :, :]
            op(pool, lambda: pool.raw.dma_start(out=dst, in_=src), reads=[hTb], writes=[hbufb[s][ti]], chan=cur["sch"])

        def norm_T(layer, ti, act_ok=True, pool_ids=range(8)):
            hT, hTb, tokA, tokAb = cur["hT"], cur["hTb"], cur["tokA"], cur["tokAb"]
            hnT, hnTb = cur["hnT"], cur["hnTb"]
            t0, N, nsb, rows = tiles[ti]
            ss = small[:, 0:4]
            rs = small[:, 4:8]
            for sb in range(nsb):
                op(act, lambda sb=sb: act.raw.activation(out=junk[:rows, :], in_=hT[:rows, sb, :], func=AF.Square,
                                                         accum_out=ss[:rows, sb:sb + 1]),
                   reads=[hTb], writes=[junkb, ss_b])
            rstd_from(ss[:rows, 0:nsb], rs[:rows, 0:nsb], float(D), ss_b, rs_b)
            for sb in range(nsb):
                if sb % 2 == 0 or not act_ok:
                    op(dve, lambda sb=sb: dve.raw.tensor_scalar(out=tokA[:rows, sb, :], in0=hT[:rows, sb, :],
                                                                scalar1=rs[:rows, sb:sb + 1], scalar2=None,
                                                                op0=ALU.mult),
                       reads=[hTb, rs_b], writes=[tokAb])
                else:
                    op(act, lambda sb=sb: act.raw.activation(out=tokA[:rows, sb, :], in_=hT[:rows, sb, :],
                                                             func=AF.Copy, scale=rs[:rows, sb:sb + 1]),
                       reads=[hTb, rs_b], writes=[tokAb])
            transpose_to(hnT, hnTb, layer, N, nsb, rows, pool_ids)

        def transpose_to(dst, dstb, layer, N, nsb, rows, pool_ids=range(8)):
            tokA, tokAb = cur["tokA"], cur["tokAb"]
            for c in range(8):
                bk, bb = next_bank(pool_ids)
                pT = bk.bitcast(BF16)
                for sb in range(nsb):
                    op(pe, lambda sb=sb: pe.raw.transpose(out=pT[:, sb * 128:sb * 128 + rows],
                                                          in_=tokA[:rows, sb, c * 128:(c + 1) * 128],
                                                          identity=identb[:rows, :rows]),
                       reads=[tokAb, constb], writes=[bb], sig=(sb == nsb - 1))
                if layer is None:
                    if c % 2 == 1:
                        op(act, lambda: act.raw.activation(out=dst[:, c, :N], in_=pT[:, :N], func=AF.Copy),
                           reads=[bb], writes=[dstb])
                    else:
                        op(dve, lambda: dve.raw.tensor_copy(out=dst[:, c, :N], in_=pT[:, :N]),
                           reads=[bb], writes=[dstb])
                else:
                    g = cvs("gT", layer * 8 + c, 1)
                    if c % 2 == 0:
                        op(dve, lambda: dve.raw.tensor_scalar(out=dst[:, c, :N], in0=pT[:, :N], scalar1=g,
                                                              scalar2=None, op0=ALU.mult),
                           reads=[bb, cvb], writes=[dstb])
                    else:
                        op(act, lambda: act.raw.activation(out=dst[:, c, :N], in_=pT[:, :N], func=AF.Copy, scale=g),
                           reads=[bb, cvb], writes=[dstb])

        def final_out(s, ti, fo, fob, foch):
            hT, hTb = cur["hT"], cur["hTb"]
            t0, N, nsb, rows = tiles[ti]
            ss = small[:, 8:12]
            rs = small[:, 12:16]
            for sb in range(nsb):
                op(act, lambda sb=sb: act.raw.activation(out=junk[:rows, :], in_=hT[:rows, sb, :], func=AF.Square,
                                                         accum_out=ss[:rows, sb:sb + 1]),
                   reads=[hTb], writes=[junkb, ss_b])
            rstd_from(ss[:rows, 0:nsb], rs[:rows, 0:nsb], float(D), ss_b, rs_b)
            for sb in range(nsb):
                op(dve, lambda sb=sb: dve.raw.scalar_tensor_tensor(out=fo[:rows, sb, :], in0=hT[:rows, sb, :],
                                                                   scalar=rs[:rows, sb:sb + 1], in1=cbs("fg")[:rows, :],
                                                                   op0=ALU.mult, op1=ALU.mult),
                   reads=[hTb, rs_b, cbb], writes=[fob])
            dst = out_d[s, t0 - NMETA:t0 - NMETA + N, :].rearrange("(s p) d -> p s d", p=128)
            op(pool, lambda: pool.raw.dma_start(out=dst, in_=fo[:, :, :]), reads=[fob], writes=[outb], chan=foch)

        def conv_layer(layer):
            lc = layer // 2
            last = layer == depth - 1
            base = 16 * lc
            with ExitStack() as ls:
                v = sbuf(f"cv_v_L{layer}", [128, 16, HALO + TS], BF16, ls)
                vb = [Buf(f"v{j}") for j in range(16)]
                vhalo = sbuf(f"cv_halo_L{layer}", [128, 16, HALO], BF16, ls)
                tsl = [sbuf(f"cv_tsl{i}_L{layer}", [128, 8192], BF16, ls) for i in range(2)]
                tp_state["slabs"] = tsl
                tp_state["bufs"] = [Buf(f"tsl{i}") for i in range(2)]
                tp_state["chans"] = [k.chan(f"tsl{i}_{layer}") for i in range(2)]
                tp_state["limit"] = (lc + 1) * n_seq * (len(tiles) - 1) * 16
                vt = [sbuf(f"cv_vt{i}_L{layer}", [128, HALO + TS], BF16, ls) for i in range(2)]
                vtb = [Buf(f"vt{i}") for i in range(2)]
                pk = [sbuf(f"cv_pk{i}_L{layer}", [128, 32], BF16, ls) for i in range(8)]
                pkb = [Buf(f"pk{i}") for i in range(8)]
                pkr = [0]
                vhb = Buf("vhalo")
                co = sbuf(f"cv_co_L{layer}", [128, 16, TS], F32, ls)
                cob = [Buf(f"co{j}") for j in range(16)]
                sz = sbuf(f"cv_sz_L{layer}", [128, 16, TS], BF16, ls)
                szb = [Buf(f"sz{j}") for j in range(16)]
                tmpf = [sbuf(f"cv_t{i}_L{layer}", [128, TS], F32, ls) for i in range(3)]
                tmpfb = [Buf(f"cv_t{i}") for i in range(3)]
                xb = [sbuf(f"cv_xb{i}_L{layer}", [128, 2, TS], BF16, ls) for i in range(2)]
                xbb = [Buf(f"cv_xb{i}") for i in range(2)]
                ones = sbuf(f"cv_ones_L{layer}", [128, 128], BF16, ls)
                onesb = Buf("ones")
                stA = sbuf(f"cv_stA_L{layer}", [128, TS], F32, ls)
                stB = sbuf(f"cv_stB_L{layer}", [128, TS], F32, ls)
                stT = sbuf(f"cv_stT_L{layer}", [128, TS], F32, ls)
                stb = Buf("stAB")
                sttb = Buf("stT")
                fo = fob = foch = None
                if last:
                    fo, fob = hT, hTb
                    foch = k.chan("fo_c%d" % layer)
                op(pool, lambda: pool.raw.memset(ones[:], 1.0), writes=[onesb])
                hT2 = sbuf(f"cv_hT2_L{layer}", [128, 4, D], F32, ls)
                hnT2 = sbuf(f"cv_hnT2_L{layer}", [128, 8, TS], BF16, ls)
                sets = [dict(cur),
                        {"hT": hT2, "hTb": Buf("hT2"), "tokA": tokA, "tokAb": tokAb, "hnT": hnT2, "hnTb": Buf("hnT2"),
                         "hch": k.chan(f"hload2_{layer}"), "sch": k.chan(f"hstore2_{layer}")}]
                all_tiles = [(s_, ti_) for s_ in range(n_seq) for ti_ in range(len(tiles))]

                def prep(idx_):
                    s_, ti_ = all_tiles[idx_]
                    cur.update(sets[idx_ % 2])
                    load_h(layer, s_, ti_)
                    norm_T(layer, ti_, pool_ids=range(6))

                prep(0)
                tile_idx = [0]
                trr = [0]

                def next_tmp():
                    i = trr[0] % 3
                    trr[0] += 1
                    return tmpf[i], tmpfb[i]

                for s in range(n_seq):
                    prevN = None
                    for ti in range(len(tiles)):
                        t0, N, nsb, rows = tiles[ti]
                        my_idx = tile_idx[0]
                        tile_idx[0] += 1
                        st_ = sets[my_idx % 2]
                        chT, chTb, chn, chnb = st_["hT"], st_["hTb"], st_["hnT"], st_["hnTb"]
                        if prevN is None:
                            op(pool, lambda: pool.raw.memset(v[:, :, 0:HALO], 0.0), writes=vb)
                        else:
                            op(pool, lambda pn=prevN: pool.raw.tensor_copy(out=vhalo[:, :, :], in_=v[:, :, pn:pn + HALO]),
                               reads=vb, writes=[vhb])
                            op(pool, lambda: pool.raw.tensor_copy(out=v[:, :, 0:HALO], in_=vhalo[:, :, :]),
                               reads=[vhb], writes=vb)
                        prevN = N
                        ps_s, ps_sb = banks[6], bankb[6]
                        ps_q, ps_qb = banks[7], bankb[7]

                        def conv_mm_prod(j):
                            pc, pcb = next_bank(range(6), claim=True)
                            for kk in range(KC):
                                w = cvs("dwk", (lc * 16 + j) * KC + kk, 1)
                                pi = pkr[0] % len(pk)
                                pkr[0] += 1
                                o0 = kk + HALO - 30
                                e0 = o0 & ~1
                                wn = min(N + 2, HALO + N - e0)
                                if (kk % 5 in (1, 3)) if PRODMODE == 0 else (PRODMODE == 1):
                                    op(act, lambda pi=pi, w=w, e0=e0, wn=wn: act.raw.activation(
                                        out=pk[pi][:, :wn], in_=v[:, j, e0:e0 + wn], func=AF.Copy, scale=w),
                                       reads=[vb[j], cvb], writes=[pkb[pi]])
                                else:
                                    op(dve, lambda pi=pi, w=w, e0=e0, wn=wn: dve.raw.tensor_scalar(
                                        out=pk[pi][:, :wn], in0=v[:, j, e0:e0 + wn], scalar1=w, scalar2=None,
                                        op0=ALU.mult),
                                       reads=[vb[j], cvb], writes=[pkb[pi]])
                                op(pe, lambda pi=pi, kk=kk, e0=e0, o0=o0: pe.raw.matmul(pc[:, :N], lhsT=identb[:, :],
                                                                                 rhs=pk[pi][:, o0 - e0:o0 - e0 + N],
                                                                                 start=(kk == 0), stop=(kk == KC - 1)),
                                   reads=[pkb[pi], constb], writes=[pcb], sig=True)
                            return pc, pcb

                        def conv_tin(j):
                            vi = j % 2
                            op(dve, lambda: dve.raw.transpose(out=vt[vi][:, :], in_=v[:, j, :]),
                               reads=[vb[j]], writes=[vtb[vi]])

                        def conv_mm_toep(j):
                            vi = j % 2
                            tslab, tslb = toep_take(lc * 16 + j)
                            tv = tslab[:, :].rearrange("p (c w m) -> p c w m", c=32, w=2)
                            pc, pcb = next_bank(range(6), claim=True)
                            pcv = pc.rearrange("p (n c) -> p n c", c=32)
                            vtv = vt[vi][:, :].rearrange("p (n c) -> p n c", c=32)
                            for cl in range(32):
                                op(pe, lambda cl=cl: pe.raw.matmul(pcv[:, :, cl], lhsT=tv[:, cl, 0, :],
                                                                   rhs=vtv[:, 1:17, cl], start=True, stop=False),
                                   reads=[tslb, vtb[vi]], writes=[pcb], sig=False)
                                op(pe, lambda cl=cl: pe.raw.matmul(pcv[:, :, cl], lhsT=tv[:, cl, 1, :],
                                                                   rhs=vtv[:, 0:16, cl], start=False, stop=True),
                                   reads=[tslb, vtb[vi]], writes=[pcb], sig=(cl == 31))
                            return pc, pcb

                        def conv_evac(j, pc, pcb, toep):
                            bank_release(pcb)
                            bsc = cvs("dwb", lc * 16 + j, 1)
                            xi = j % 2
                            if toep:
                                op(dve, lambda: dve.raw.transpose(out=co[:, j, :], in_=pc[:, :]),
                                   reads=[pcb], writes=[cob[j]])
                                op(dve, lambda: dve.raw.tensor_scalar(out=co[:, j, :N], in0=co[:, j, :N], scalar1=bsc,
                                                                      scalar2=None, op0=ALU.add),
                                   reads=[cob[j], cvb], writes=[cob[j]])
                            else:
                                op(dve, lambda: dve.raw.tensor_scalar(out=co[:, j, :N], in0=pc[:, :N], scalar1=bsc,
                                                                      scalar2=None, op0=ALU.add),
                                   reads=[pcb, cvb], writes=[cob[j]])
                            op(act, lambda: act.raw.activation(out=xb[xi][:, 0, :N], in_=co[:, j, :N], func=AF.Copy),
                               reads=[cob[j]], writes=[xbb[xi]])
                            op(act, lambda: act.raw.activation(out=xb[xi][:, 1, :N], in_=co[:, j, :N], func=AF.Square),
                               reads=[cob[j]], writes=[xbb[xi]])
                            op(pe, lambda: pe.raw.matmul(ps_s[:, :N], lhsT=ones[:, :], rhs=xb[xi][:, 0, :N],
                                                         start=(j == 0), stop=(j == 15)),
                               reads=[onesb, xbb[xi]], writes=[ps_sb], sig=False)
                            op(pe, lambda: pe.raw.matmul(ps_q[:, :N], lhsT=ones[:, :], rhs=xb[xi][:, 1, :N],
                                                         start=(j == 0), stop=(j == 15)),
                               reads=[onesb, xbb[xi]], writes=[ps_qb], sig=True)


                        use_toep = ti > 0 and DBG not in (5, 6)
                        pend_ev = []

                        def conv_chunk(j):
                            if use_toep:
                                conv_tin(j)
                                pc, pcb = conv_mm_toep(j)
                            else:
                                pc, pcb = conv_mm_prod(j)
                            while pend_ev:
                                conv_evac(*pend_ev.pop(0))
                            pend_ev.append((j, pc, pcb, use_toep))
                            if DBG == 6:
                                conv_evac(*pend_ev.pop(0))

                        pend = {}

                        slab_cur = {}

                        def proj_chunk(j):
                            sp_i, jj = j // 2, j % 2
                            if jj == 0:
                                slab, slb = slab_take(base + sp_i)
                                slab_cur["sl"] = slab[:, :].rearrange("p (a w) -> p a w", a=8)
                                slab_cur["b"] = slb
                            sl, slb = slab_cur["sl"], slab_cur["b"]
                            pu, pub = next_bank(range(6), claim=True)
                            pg, pgb = next_bank(range(6), claim=True)
                            for dk in range(8):
                                op(pe, lambda dk=dk: pe.raw.matmul(pu[:, :N], lhsT=sl[:, dk, jj * 128:(jj + 1) * 128],
                                                                   rhs=chn[:, dk, :N], start=(dk == 0), stop=(dk == 7)),
                                   reads=[slb, chnb], writes=[pub], sig=(dk == 7))
                            for dk in range(8):
                                op(pe, lambda dk=dk: pe.raw.matmul(pg[:, :N],
                                                                   lhsT=sl[:, dk, 256 + jj * 128:256 + (jj + 1) * 128],
                                                                   rhs=chn[:, dk, :N], start=(dk == 0), stop=(dk == 7)),
                                   reads=[slb, chnb], writes=[pgb], sig=(dk == 7))
                            pend[j] = (pu, pub, pg, pgb)

                        def glu_chunk(j):
                            pu, pub, pg, pgb = pend.pop(j)
                            bank_release(pub)
                            bank_release(pgb)
                            sg, sgb = next_tmp()
                            op(act, lambda: act.raw.activation(out=sg[:, :N], in_=pg[:, :N], func=AF.Sigmoid),
                               reads=[pgb], writes=[sgb])
                            op(dve, lambda: dve.raw.tensor_tensor(out=v[:, j, HALO:HALO + N], in0=pu[:, :N],
                                                                  in1=sg[:, :N], op=ALU.mult),
                               reads=[pub, sgb], writes=[vb[j]])

                        def z_group(sp_i):
                            slab, slb = slab_take(base + 8 + sp_i)
                            sl = slab[:, :].rearrange("p (a w) -> p a w", a=8)
                            for jj in range(4):
                                j = 4 * sp_i + jj
                                pz, pzb = next_bank(range(6))
                                for dk in range(8):
                                    op(pe, lambda dk=dk: pe.raw.matmul(pz[:, :N], lhsT=sl[:, dk, jj * 128:(jj + 1) * 128],
                                                                       rhs=chn[:, dk, :N], start=(dk == 0), stop=(dk == 7)),
                                       reads=[slb, chnb], writes=[pzb], sig=(dk == 7))
                                op(act, lambda j=j: act.raw.activation(out=sz[:, j, :N], in_=pz[:, :N], func=AF.Silu),
                                   reads=[pzb], writes=[szb[j]])

                        proj_chunk(0)
                        for j in range(16):
                            if j + 1 < 16:
                                proj_chunk(j + 1)
                            glu_chunk(j)
                            conv_chunk(j)
                        z_group(0)
                        while pend_ev:
                            conv_evac(*pend_ev.pop(0))
                        for sp_i in range(1, 4):
                            z_group(sp_i)
                        if my_idx + 1 < len(all_tiles):
                            prep(my_idx + 1)
                        cur.update(st_)
                        op(dve, lambda: dve.raw.tensor_scalar(out=stT[:, :N], in0=ps_s[:, :N], scalar1=1.0 / CW,
                                                              scalar2=None, op0=ALU.mult),
                           reads=[ps_sb], writes=[sttb])
                        op(dve, lambda: dve.raw.tensor_tensor(out=stB[:, :N], in0=stT[:, :N], in1=stT[:, :N], op=ALU.mult),
                           reads=[sttb], writes=[stb])
                        op(dve, lambda: dve.raw.scalar_tensor_tensor(out=stA[:, :N], in0=ps_q[:, :N], scalar=1.0 / CW,
                                                                     in1=stB[:, :N], op0=ALU.mult, op1=ALU.subtract),
                           reads=[ps_qb, stb], writes=[stb])
                        op(dve, lambda: dve.raw.tensor_scalar(out=stA[:, :N], in0=stA[:, :N], scalar1=EPS, scalar2=None,
                                                              op0=ALU.add), reads=[stb], writes=[stb])
                        op(act, lambda: act.raw.activation(out=stA[:, :N], in_=stA[:, :N], func=AF.Ln),
                           reads=[stb], writes=[stb])
                        op(act, lambda: act.raw.activation(out=stA[:, :N], in_=stA[:, :N], func=AF.Exp, scale=-0.5),
                           reads=[stb], writes=[stb])
                        op(dve, lambda: dve.raw.scalar_tensor_tensor(out=stB[:, :N], in0=stT[:, :N], scalar=-1.0,
                                                                     in1=stA[:, :N], op0=ALU.mult, op1=ALU.mult),
                           reads=[sttb, stb], writes=[stb])
                        pend_g = None
                        for j in range(16):
                            t1, t1b = next_tmp()
                            e1 = pool if j % 4 == 3 else dve
                            op(e1, lambda j=j, t1=t1, e1=e1: e1.raw.tensor_tensor(out=t1[:, :N], in0=co[:, j, :N],
                                                                                 in1=stA[:, :N], op=ALU.mult),
                               reads=[cob[j], stb], writes=[t1b])
                            op(e1, lambda t1=t1, e1=e1: e1.raw.tensor_tensor(out=t1[:, :N], in0=t1[:, :N], in1=stB[:, :N],
                                                                            op=ALU.add),
                               reads=[t1b, stb], writes=[t1b])
                            op(act, lambda j=j, t1=t1: act.raw.activation(out=t1[:, :N], in_=t1[:, :N], func=AF.Silu,
                                                                          scale=cvs("lng", lc * 16 + j, 1),
                                                                          bias=cvs("lnb", lc * 16 + j, 1)),
                               reads=[t1b, cvb], writes=[t1b])
                            if pend_g is not None:
                                pj, pt1, pt1b = pend_g
                                op(dve, lambda pj=pj, pt1=pt1: dve.raw.tensor_tensor(out=sz[:, pj, :N], in0=pt1[:, :N],
                                                                                     in1=sz[:, pj, :N], op=ALU.mult),
                                   reads=[pt1b, szb[pj]], writes=[szb[pj]])
                            pend_g = (j, t1, t1b)
                        pj, pt1, pt1b = pend_g
                        op(dve, lambda: dve.raw.tensor_tensor(out=sz[:, pj, :N], in0=pt1[:, :N], in1=sz[:, pj, :N],
                                                              op=ALU.mult),
                           reads=[pt1b, szb[pj]], writes=[szb[pj]])
                        for sp_i in range(4):
                            slab, slb = slab_take(base + 12 + sp_i)
                            sl = slab[:, :].rearrange("p (a w) -> p a w", a=16)
                            for sb in range(nsb):
                                po, pob = next_bank(range(6))
                                for ck in range(16):
                                    op(pe, lambda ck=ck, sb=sb: pe.raw.matmul(po[:rows, 0:256],
                                                                              lhsT=sz[:, ck, sb * 128:sb * 128 + rows],
                                                                              rhs=sl[:, ck, :], start=(ck == 0),
                                                                              stop=(ck == 15)),
                                       reads=[slb, szb[ck]], writes=[pob], sig=(ck == 15))
                                dsl = slice(256 * sp_i, 256 * sp_i + 256)
                                op(dve, lambda sb=sb, dsl=dsl: dve.raw.tensor_tensor(out=chT[:rows, sb, dsl],
                                                                                     in0=chT[:rows, sb, dsl],
                                                                                     in1=po[:rows, 0:256], op=ALU.add),
                                   reads=[pob, chTb], writes=[chTb])
                        if last:
                            if ti > 0:
                                final_out(s, ti, chT, chTb, foch)
                        else:
                            store_h(s, ti)
                cur.update(sets[0])
                k.barrier()

        def attn_layer(layer):
            la = layer // 2
            last = layer == depth - 1
            base = attn_base(la)
            lam_init = 0.8 - 0.6 * math.exp(-0.3 * layer)
            with ExitStack() as ls:
                kT = sbuf(f"at_kT_L{layer}", [128, NH, L], BF16, ls)
                kTb = Buf("kT")
                qT = sbuf(f"at_qT_L{layer}", [128, NH, TS], BF16, ls)
                qTb = Buf("qT")
                szg = sbuf(f"at_szg_L{layer}", [128, 4, D], BF16, ls)
                szgb = Buf("szg")
                vbuf = [sbuf(f"at_vb{i}_L{layer}", [128, NKB, 129], BF16, ls) for i in range(2)]
                vbufb = [Buf(f"vb{i}") for i in range(2)]
                vbch = [k.chan(f"vb{i}_{layer}") for i in range(2)]
                vst = [sbuf(f"at_vst{i}_L{layer}", [128, NH, 129], BF16, ls) for i in range(4)]
                vstb = [Buf(f"vst{i}") for i in range(4)]
                vsch = [k.chan("vst%d_%d" % (i, layer)) for i in range(4)]
                vdb = [Buf(f"vdram{i}") for i in range(NKB)]
                pTs = [sbuf(f"at_pT{i}_L{layer}", [128, 2, TS], BF16, ls) for i in range(3)]
                pTb = [Buf(f"pT{i}") for i in range(3)]
                accS = sbuf(f"at_accS_L{layer}", [128, 3, 387], F32, ls)
                accSb = Buf("accS")
                neghalf = sbuf(f"at_nh_L{layer}", [128, 4], F32, ls)
                nhb = Buf("neghalf")
                oo = [sbuf(f"at_oo{i}_L{layer}", [128, 128], F32, ls) for i in range(4)]
                oob = [Buf(f"oo{i}") for i in range(4)]
                tt = sbuf(f"at_tt_L{layer}", [128, 128], F32, ls)
                ttb = Buf("tt")
                szf = [sbuf(f"at_szf{i}_L{layer}", [128, TS], F32, ls) for i in range(2)]
                szfb = [Buf(f"szf{i}") for i in range(2)]
                gsub = sbuf(f"at_gsub_L{layer}", [128, D], F32, ls)
                gsubb = Buf("gsub")
                lam_t = sbuf(f"at_lam_L{layer}", [128, 8], F32, ls)
                lamb = Buf("lam")
                hnT_alt = sbuf(f"at_hnT2_L{layer}", [128, 8, TS], BF16, ls)
                hn_bufs = [(hnT, hnTb), (hnT_alt, Buf("hnT_alt"))]
                hn_ch = [k.chan(f"hnl{i}_{layer}") for i in range(2)]
                hnd_st = [k.chan(f"hnst{i}_{layer}") for i in range(2)]
                hndb = [Buf(f"hnd{i}") for i in range(len(tiles))]
                hs = sbuf(f"at_hs_L{layer}", [128, 32], F32, ls)
                hsb = Buf("hs")
                fo = fob = foch = None
                if last:
                    fo, fob = hT, hTb
                    foch = k.chan("fo_a%d" % layer)

                op(pool, lambda: pool.raw.memset(neghalf[:, :], -0.5), writes=[nhb])
                for i in range(4):
                    op(pool, lambda i=i: pool.raw.memset(vst[i][:, :, 128:129], 1.0), writes=[vstb[i]])
                lv = lambda i: cbs("lamv", (la * 4 + i) * 64, 64)
                op(dve, lambda: dve.raw.tensor_tensor(out=tt[:, 0:64], in0=lv(0), in1=lv(1), op=ALU.mult),
                   reads=[cbb], writes=[ttb])
                op(dve, lambda: dve.raw.tensor_tensor(out=tt[:, 64:128], in0=lv(2), in1=lv(3), op=ALU.mult),
                   reads=[cbb], writes=[ttb])
                op(dve, lambda: dve.raw.tensor_reduce(out=lam_t[:, 0:2], in_=tt[:, :].rearrange("p (a b) -> p a b", a=2),
                                                      axis=AX.X, op=ALU.add),
                   reads=[ttb], writes=[lamb])
                op(act, lambda: act.raw.activation(out=lam_t[:, 2:4], in_=lam_t[:, 0:2], func=AF.Exp),
                   reads=[lamb], writes=[lamb])
                op(dve, lambda: dve.raw.tensor_tensor(out=lam_t[:, 4:5], in0=lam_t[:, 2:3], in1=lam_t[:, 3:4],
                                                      op=ALU.subtract), reads=[lamb], writes=[lamb])
                op(dve, lambda: dve.raw.tensor_scalar(out=lam_t[:, 5:6], in0=lam_t[:, 4:5], scalar1=lam_init,
                                                      scalar2=None, op0=ALU.add), reads=[lamb], writes=[lamb])
                lam_ap = lam_t[:, 5:6]
                gsch = k.chan(f"gsub_{layer}")
                op(sp, lambda: sp.raw.dma_start(out=gsub[:, :], in_=sg_d[:, la * D:(la + 1) * D]),
                   writes=[gsubb], chan=gsch)
                op(dve, lambda: dve.raw.tensor_scalar(out=gsub[:, :], in0=gsub[:, :],
                                                      scalar1=1.0 - lam_init, scalar2=None, op0=ALU.mult),
                   reads=[gsubb], writes=[gsubb])

                prr = [0]
                accA = [(banks[2 + i], bankb[2 + i]) for i in range(3)]
                accB = [(banks[5 + i], bankb[5 + i]) for i in range(3)]

                def acc_region(aset, m, sb):
                    idx = m * 4 + sb
                    bk, bb = aset[idx // 3]
                    c = (idx % 3) * 129
                    return bk, bb, c

                for s in range(n_seq):
                    def prep1(ti_, par_):
                        cur.update({"hnT": hn_bufs[par_][0], "hnTb": hn_bufs[par_][1]})
                        load_h(layer, s, ti_)
                        norm_T(layer, ti_)
                        n_ = tiles[ti_][1]
                        op(sp, lambda: sp.raw.dma_start(out=hnd[ti_, :, :, 0:n_], in_=hn_bufs[par_][0][:, :, 0:n_]),
                           reads=[hn_bufs[par_][1]], writes=[hndb[ti_]], chan=hnd_st[par_])

                    prep1(0, 0)
                    for ti in range(len(tiles)):
                        t0, N, nsb, rows = tiles[ti]
                        par1 = ti % 2
                        if ti + 1 < len(tiles):
                            prep1(ti + 1, 1 - par1)
                        khn, khnb = hn_bufs[par1]
                        for half in range(2):
                            slab, slb = slab_take(base + 2 + half)
                            sl = slab[:, :].rearrange("p (a w) -> p a w", a=8)
                            for hh in range(4):
                                h = 4 * half + hh
                                pk, pkb = next_bank()
                                for dk in range(8):
                                    op(pe, lambda dk=dk: pe.raw.matmul(pk[:, :N], lhsT=sl[:, dk, hh * 128:(hh + 1) * 128],
                                                                       rhs=khn[:, dk, :N], start=(dk == 0), stop=(dk == 7)),
                                       reads=[slb, khnb], writes=[pkb], sig=(dk == 7))
                                if h % 2 == 0:
                                    op(act, lambda h=h: act.raw.activation(out=kT[:, h, t0:t0 + N], in_=pk[:, :N],
                                                                           func=AF.Copy),
                                       reads=[pkb], writes=[kTb])
                                else:
                                    op(dve, lambda h=h: dve.raw.tensor_copy(out=kT[:, h, t0:t0 + N], in_=pk[:, :N]),
                                       reads=[pkb], writes=[kTb])
                        for half in range(2):
                            slab, slb = slab_take(base + 4 + half)
                            sl = slab[:, :].rearrange("p (a w) -> p a w", a=8)
                            for sb in range(nsb):
                                pv, pvb = next_bank()
                                for dk in range(8):
                                    op(pe, lambda dk=dk, sb=sb: pe.raw.matmul(pv[:rows, :],
                                                                              lhsT=khn[:, dk, sb * 128:sb * 128 + rows],
                                                                              rhs=sl[:, dk, :], start=(dk == 0),
                                                                              stop=(dk == 7)),
                                       reads=[slb, khnb], writes=[pvb], sig=(dk == 7))
                                kb = 0 if ti == 0 else 1 + 4 * (ti - 1) + sb
                                vi = sb
                                src = pv[:rows, :].rearrange("p (h e) -> p h e", h=4)
                                e = act if sb % 2 == 0 else dve
                                if e is act:
                                    op(act, lambda vi=vi, src=src: act.raw.activation(
                                        out=vst[vi][:rows, 4 * half:4 * half + 4, 0:128], in_=src, func=AF.Copy),
                                       reads=[pvb], writes=[vstb[vi]])
                                else:
                                    op(dve, lambda vi=vi, src=src: dve.raw.tensor_copy(
                                        out=vst[vi][:rows, 4 * half:4 * half + 4, 0:128], in_=src),
                                       reads=[pvb], writes=[vstb[vi]])
                                if half == 1:
                                    op(sp, lambda kb=kb, vi=vi: sp.raw.dma_start(out=vdram[kb, :rows, :, :],
                                                                                 in_=vst[vi][:rows, :, :]),
                                       reads=[vstb[vi]], writes=[vdb[kb]], chan=vsch[vi])
                    cur.update({"hnT": hn_bufs[0][0], "hnTb": hn_bufs[0][1]})
                    p2_tiles = [ti for ti in range(len(tiles)) if not (last and ti == 0)]

                    def load_hn(ti, par):
                        n_ = tiles[ti][1]
                        op(sp, lambda: sp.raw.dma_start(out=hn_bufs[par][0][:, :, 0:n_], in_=hnd[ti, :, :, 0:n_]),
                           reads=[hndb[ti]], writes=[hn_bufs[par][1]], chan=hn_ch[par])

                    load_hn(p2_tiles[0], 0)
                    for pidx_, ti in enumerate(p2_tiles):
                        t0, N, nsb, rows = tiles[ti]
                        par = pidx_ % 2
                        chn, chnb = hn_bufs[par]
                        if pidx_ + 1 < len(p2_tiles):
                            load_hn(p2_tiles[pidx_ + 1], 1 - par)
                        load_h(layer, s, ti)
                        for half in range(2):
                            slab, slb = slab_take(base + 0 + half)
                            sl = slab[:, :].rearrange("p (a w) -> p a w", a=8)
                            for hh in range(4):
                                h = 4 * half + hh
                                pq, pqb = next_bank([0, 1, 2, 3, 7])
                                for dk in range(8):
                                    op(pe, lambda dk=dk: pe.raw.matmul(pq[:, :N], lhsT=sl[:, dk, hh * 128:(hh + 1) * 128],
                                                                       rhs=chn[:, dk, :N], start=(dk == 0), stop=(dk == 7)),
                                       reads=[slb, chnb], writes=[pqb], sig=(dk == 7))
                                op(dve, lambda h=h: dve.raw.tensor_copy(out=qT[:, h, :N], in_=pq[:, :N]),
                                   reads=[pqb], writes=[qTb])
                        for half in range(2):
                            slab, slb = slab_take(base + 6 + half)
                            sl = slab[:, :].rearrange("p (a w) -> p a w", a=8)
                            for sb in range(nsb):
                                pz, pzb = next_bank([0, 1, 2, 3, 7])
                                for dk in range(8):
                                    op(pe, lambda dk=dk, sb=sb: pe.raw.matmul(pz[:rows, :],
                                                                              lhsT=chn[:, dk, sb * 128:sb * 128 + rows],
                                                                              rhs=sl[:, dk, :], start=(dk == 0),
                                                                              stop=(dk == 7)),
                                       reads=[slb, chnb], writes=[pzb], sig=(dk == 7))
                                zi = sb % 2
                                op(act, lambda zi=zi: act.raw.activation(out=szf[zi][:rows, :], in_=pz[:rows, :],
                                                                         func=AF.Silu),
                                   reads=[pzb], writes=[szfb[zi]])
                                op(dve, lambda zi=zi, sb=sb: dve.raw.tensor_tensor(
                                    out=szg[:rows, sb, half * 512:(half + 1) * 512], in0=szf[zi][:rows, :],
                                    in1=gsub[:rows, half * 512:(half + 1) * 512], op=ALU.mult),
                                   reads=[szfb[zi], gsubb], writes=[szgb])
                        nkb = 1 if ti == 0 else 4 * ti + 1
                        first_kb = 0 if ti == 0 else 4 * (ti - 1) + 1

                        def load_v(h):
                            vi = h % 2
                            op(sp, lambda: sp.raw.dma_start(out=vbuf[vi][:NMETA, 0, :], in_=vdram[0, :NMETA, h, :]),
                               reads=vdb[0:1], writes=[vbufb[vi]], chan=vbch[vi])
                            if nkb > 1:
                                src = vdram[1:nkb, :, h, :].rearrange("k p e -> p k e")
                                op(sp, lambda: sp.raw.dma_start(out=vbuf[vi][:, 1:nkb, :], in_=src),
                                   reads=vdb[1:nkb], writes=[vbufb[vi]], chan=vbch[vi])

                        def geom(kb):
                            rows_k = NMETA if kb == 0 else 128
                            kp0 = 0 if kb == 0 else NMETA + 128 * (kb - 1)
                            diag = kb >= first_kb
                            c0 = (kb - first_kb) * 128 if (diag and ti > 0) else 0
                            return rows_k, kp0, diag, c0

                        def emit_qk(h, kb, pair):
                            rows_k, kp0, diag, c0 = geom(kb)
                            for m in range(2):
                                st = banks[2 * pair + m]
                                stb_ = bankb[2 * pair + m]
                                op(pe, lambda m=m, st=st: pe.raw.matmul(st[:rows_k, c0:N],
                                                                        lhsT=kT[64 * m:64 * m + 64, h, kp0:kp0 + rows_k],
                                                                        rhs=qT[64 * m:64 * m + 64, h, c0:N],
                                                                        start=True, stop=(not diag)),
                                   reads=[kTb, qTb], writes=[stb_], sig=(not diag))
                                if diag:
                                    w = min(128, N - c0)
                                    op(pe, lambda st=st, w=w: pe.raw.matmul(st[:rows_k, c0:c0 + w],
                                                                            lhsT=identb[:rows_k, :rows_k],
                                                                            rhs=maskb[:rows_k, :w], start=False, stop=True),
                                       reads=[constb], writes=[stb_], sig=True)

                        def emit_exp(h, kb, pair, pi):
                            rows_k, kp0, diag, c0 = geom(kb)
                            gs = min(GS[h], N)
                            pt, ptb = pTs[pi], pTb[pi]
                            for g in range((N + gs - 1) // gs):
                                a_ = max(c0, g * gs)
                                b_ = min(N, (g + 1) * gs)
                                if b_ <= a_:
                                    continue
                                if ti == 0:
                                    bias = cvs("T2", h * 34 + 33, 1)
                                elif kb == 0:
                                    n_ = 4 * (ti - 1) + (g * gs) // 128
                                    bias = cvs("T2", h * 34 + n_, 1)
                                else:
                                    delta = (kb - 1) - 4 * (ti - 1) - (g * gs) // 128
                                    bias = cvs("T1", h * 36 + delta + 32, 1)
                                op(act, lambda a_=a_, b_=b_, bias=bias: act.raw.activation(
                                    out=pt[:rows_k, :, a_:b_], in_=ps_all[:rows_k, 2 * pair:2 * pair + 2, a_:b_],
                                    func=AF.Exp, bias=bias[:rows_k, :], scale=0.125),
                                   reads=[bankb[2 * pair], bankb[2 * pair + 1], cvb], writes=[ptb])

                        def pv_plan():
                            plan = []
                            for kb in range(nkb):
                                c0_ = geom(kb)[3]
                                for m in range(2):
                                    for sb in range(nsb):
                                        if sb * 128 < c0_:
                                            continue
                                        plan.append((kb, m, sb, (m * 4 + sb) // 3))
                            first_in, last_in = {}, {}
                            for i_, (kb_, m_, sb_, bi_) in enumerate(plan):
                                first_in.setdefault(bi_, i_)
                                last_in[bi_] = i_
                            return plan, first_in, last_in

                        plan, first_in, last_in = pv_plan()
                        plan_by_kb = {}
                        for i_, (kb_, m_, sb_, bi_) in enumerate(plan):
                            plan_by_kb.setdefault(kb_, []).append((i_, m_, sb_, bi_))

                        def emit_pv(h, kb, pi):
                            rows_k = geom(kb)[0]
                            vi = h % 2
                            pt, ptb = pTs[pi], pTb[pi]
                            for (i_, m, sb, bi_) in plan_by_kb[kb]:
                                c = ((m * 4 + sb) % 3) * 129
                                bk, bb = banks[4 + bi_], bankb[4 + bi_]
                                op(pe, lambda m=m, sb=sb, bk=bk, c=c, i_=i_, bi_=bi_: pe.raw.matmul(
                                    bk[:rows, c:c + 129], lhsT=pt[:rows_k, m, sb * 128:sb * 128 + rows],
                                    rhs=vbuf[vi][:rows_k, kb, :], start=(first_in[bi_] == i_),
                                    stop=(last_in[bi_] == i_)),
                                   reads=[ptb, vbufb[vi]], writes=[bb], sig=True)

                        def emit_post(h):
                            for bi_ in range(3):
                                op(dve, lambda bi_=bi_: dve.raw.tensor_copy(out=accS[:rows, bi_, :],
                                                                            in_=ps_all[:rows, 4 + bi_, 0:387]),
                                   reads=[bankb[4 + bi_]], writes=[accSb])
                            rec = hs[:, 0:8]
                            r1l = hs[:, 8:12]
                            ssq = hs[:, 12:16]
                            rsd = hs[:, 16:20]
                            tq = hs[:, 20:24]

                            def areg(m, sb):
                                idx = m * 4 + sb
                                return accS[:rows, idx // 3, (idx % 3) * 129:(idx % 3) * 129 + 129]
                            for m in range(2):
                                for sb in range(nsb):
                                    op(dve, lambda m=m, sb=sb: dve.raw.reciprocal(
                                        out=rec[:rows, m * 4 + sb:m * 4 + sb + 1], in_=areg(m, sb)[:, 128:129]),
                                       reads=[accSb], writes=[hsb])
                            op(dve, lambda: dve.raw.tensor_scalar(out=r1l[:rows, 0:nsb], in0=rec[:rows, 4:4 + nsb],
                                                                  scalar1=lam_ap[:rows, :], scalar2=None, op0=ALU.mult),
                               reads=[hsb, lamb], writes=[hsb])
                            for sb in range(nsb):
                                op(dve, lambda sb=sb: dve.raw.tensor_scalar(
                                    out=tt[:rows, :], in0=areg(1, sb)[:, 0:128], scalar1=r1l[:rows, sb:sb + 1],
                                    scalar2=None, op0=ALU.mult),
                                   reads=[accSb, hsb], writes=[ttb])
                                op(dve, lambda sb=sb: dve.raw.scalar_tensor_tensor(
                                    out=oo[sb][:rows, :], in0=areg(0, sb)[:, 0:128], scalar=rec[:rows, sb:sb + 1],
                                    in1=tt[:rows, :], op0=ALU.mult, op1=ALU.subtract),
                                   reads=[accSb, hsb, ttb], writes=[oob[sb]])
                                op(dve, lambda sb=sb: dve.raw.scalar_tensor_tensor(
                                    out=tt[:rows, :], in0=oo[sb][:rows, :], scalar=1.0, in1=oo[sb][:rows, :],
                                    op0=ALU.mult, op1=ALU.mult, accum_out=ssq[:rows, sb:sb + 1]),
                                   reads=[oob[sb]], writes=[ttb, hsb])
                            op(dve, lambda: dve.raw.tensor_scalar(out=tq[:rows, 0:nsb], in0=ssq[:rows, 0:nsb],
                                                                  scalar1=1.0 / 128.0, scalar2=EPS, op0=ALU.mult,
                                                                  op1=ALU.add), reads=[hsb], writes=[hsb])
                            op(pool, lambda: pool.raw.tensor_tensor(out=rsd[:rows, 0:nsb], in0=tq[:rows, 0:nsb],
                                                                    in1=neghalf[:rows, 0:nsb], op=ALU.pow),
                               reads=[hsb, nhb], writes=[hsb])
                            for sb in range(nsb):
                                op(dve, lambda sb=sb: dve.raw.scalar_tensor_tensor(
                                    out=tokA[:rows, sb, h * 128:(h + 1) * 128], in0=oo[sb][:rows, :],
                                    scalar=rsd[:rows, sb:sb + 1], in1=szg[:rows, sb, h * 128:(h + 1) * 128],
                                    op0=ALU.mult, op1=ALU.mult),
                                   reads=[oob[sb], hsb, szgb], writes=[tokAb])

                        items = [(h, kb) for h in range(NH) for kb in range(nkb)]
                        load_v(0)
                        emit_qk(items[0][0], items[0][1], 0)
                        for i, (h, kb) in enumerate(items):
                            if kb == 0 and h + 1 < NH:
                                load_v(h + 1)
                            if i + 1 < len(items):
                                emit_qk(items[i + 1][0], items[i + 1][1], (i + 1) % 2)
                            pi = prr[0] % len(pTs)
                            prr[0] += 1
                            emit_exp(h, kb, i % 2, pi)
                            emit_pv(h, kb, pi)
                            if kb == nkb - 1:
                                emit_post(h)
                        transpose_to(chn, chnb, None, N, nsb, rows)
                        for half in range(2):
                            slab, slb = slab_take(base + 8 + half)
                            sl = slab[:, :].rearrange("p (a w) -> p a w", a=8)
                            for sb in range(nsb):
                                po, pob = next_bank([0, 1, 2, 3, 7])
                                for ck in range(8):
                                    op(pe, lambda ck=ck, sb=sb: pe.raw.matmul(po[:rows, :],
                                                                              lhsT=chn[:, ck, sb * 128:sb * 128 + rows],
                                                                              rhs=sl[:, ck, :], start=(ck == 0),
                                                                              stop=(ck == 7)),
                                       reads=[slb, chnb], writes=[pob], sig=(ck == 7))
                                dsl = slice(512 * half, 512 * half + 512)
                                op(dve, lambda sb=sb, dsl=dsl: dve.raw.tensor_tensor(out=hT[:rows, sb, dsl],
                                                                                     in0=hT[:rows, sb, dsl],
                                                                                     in1=po[:rows, :], op=ALU.add),
                                   reads=[pob, hTb], writes=[hTb])
                        if last:
                            final_out(s, ti, fo, fob, foch)
                        else:
                            store_h(s, ti)
                k.barrier()

        for layer in range(depth):
            if layer % 2 == 0:
                conv_layer(layer)
            else:
                attn_layer(layer)
        assert sl_state["taken"] == len(slab_seq), (sl_state, len(slab_seq))
        k.barrier()
    return nc


_NC_CACHE = {}


def kernel(x, meta_tokens, norm_gain, final_norm_gain,
           conv_w_in, conv_dw_kernel, conv_dw_bias, conv_ln_gain, conv_ln_bias, conv_w_out,
           attn_w_in, attn_lambda_q1, attn_lambda_k1, attn_lambda_q2, attn_lambda_k2,
           attn_subln_gain, attn_w_out):
    n_cores = 8
    x = np.asarray(x, np.float32)
    B, SEQ, _ = x.shape
    n_seq = B // n_cores
    n_xt = SEQ // TS
    inp = dict(norm_gain=norm_gain, final_norm_gain=final_norm_gain, conv_dw_kernel=conv_dw_kernel,
               conv_dw_bias=conv_dw_bias, conv_ln_gain=conv_ln_gain, conv_ln_bias=conv_ln_bias,
               attn_lambda_q1=attn_lambda_q1, attn_lambda_k1=attn_lambda_k1, attn_lambda_q2=attn_lambda_q2,
               attn_lambda_k2=attn_lambda_k2, attn_subln_gain=attn_subln_gain)
    cv, cb = host_consts(inp)
    depth = int(np.asarray(norm_gain).shape[0])
    key = (n_seq, n_xt, depth)
    if key not in _NC_CACHE:
        _NC_CACHE[key] = build_program(n_seq, n_xt, depth)
    nc = _NC_CACHE[key]
    common = {
        "meta": np.ascontiguousarray(np.asarray(meta_tokens, np.float32)),
        "cv": cv, "cb": cb,
        "cwi": np.ascontiguousarray(np.asarray(conv_w_in, np.float32)),
        "cwo": np.ascontiguousarray(np.asarray(conv_w_out, np.float32)),
        "awi": np.ascontiguousarray(np.asarray(attn_w_in, np.float32)),
        "awo": np.ascontiguousarray(np.asarray(attn_w_out, np.float32)),
        "tz": host_toeplitz(conv_dw_kernel),
        "sgb": host_subln(inp),
    }
    in_maps = []
    for c in range(n_cores):
        d = dict(common)
        d["x"] = np.ascontiguousarray(x[c * n_seq:(c + 1) * n_seq])
        in_maps.append(d)
    res = run_bass_kernel_spmd(nc, in_maps, core_ids=list(range(n_cores)))
    out = np.concatenate([np.asarray(r["out"], np.float32) for r in res.results], axis=0)
    return out
```
